# Optimizing a Trainium2 kernel written in Bass

```python
import math
import jax, jax.numpy as jnp
from jax import lax
import numpy as np

D_MODEL = 1024
BATCH = 2
SEQ = 8192
DEPTH = 4
DEC_BATCH = 128
DEC_SEQ = 1
PAST_LEN = 8192
PAGE_SIZE = 128

N_MIXERS = 3
POOL_WINDOWS = (2, 4, 8, 16)
N_POOL_GROUPS = len(POOL_WINDOWS)
POOL_WIDTH = D_MODEL
POOL_GROUP = POOL_WIDTH // N_POOL_GROUPS
POOL_STATE = max(POOL_WINDOWS) - 1
HEAD_DIM = 64
N_HEADS = D_MODEL // HEAD_DIM
N_KV_HEADS = 4
GROUP = N_HEADS // N_KV_HEADS
ATT_WIDTH = N_HEADS * HEAD_DIM
KV_WIDTH = N_KV_HEADS * HEAD_DIM
SCALE = HEAD_DIM ** -0.5
WINDOW = 128
BLOCK = 128
N_BUCKETS = 32
MAX_DISTANCE = 128
EPS = 1e-6
NEG = -1e30
N_A = (DEPTH + 2) // N_MIXERS
N_B = (DEPTH + 1) // N_MIXERS
N_C = DEPTH // N_MIXERS
IN_A = 2 * POOL_WIDTH
IN_B = 2 * ATT_WIDTH + 2 * KV_WIDTH
IN_C = 2 * ATT_WIDTH + 2 * KV_WIDTH + N_HEADS

kernel_name = 'pool_swa_fox_hybrid_step'


def rmsnorm(x, g):
    xf = x.astype(jnp.float32)
    y = xf * lax.rsqrt(jnp.mean(xf * xf, axis=-1, keepdims=True) + EPS) * g.astype(jnp.float32)
    return y.astype(x.dtype)


def t5_bucket(dist):
    n = jnp.maximum(dist, 0)
    max_exact = N_BUCKETS // 2
    nf = jnp.maximum(n, 1).astype(jnp.float32)
    large = max_exact + (jnp.log(nf / max_exact) / math.log(MAX_DISTANCE / max_exact) * (N_BUCKETS - max_exact)).astype(jnp.int32)
    large = jnp.minimum(large, N_BUCKETS - 1)
    return jnp.where(n < max_exact, n, large)


def heads_first(c):
    b, l, _ = c.shape
    return c.reshape(b, l, N_KV_HEADS, GROUP).transpose(0, 2, 3, 1)


def multiscale_pool(u_ext, n_prefix):
    b, l, _ = u_ext.shape
    uf = u_ext.astype(jnp.float32)
    cs = jnp.concatenate([jnp.zeros((b, 1, POOL_WIDTH), jnp.float32), jnp.cumsum(uf, axis=1)], axis=1)
    rows = jnp.arange(n_prefix, l)
    hi = cs[:, rows + 1]
    outs = []
    for g, w in enumerate(POOL_WINDOWS):
        lo = jnp.maximum(rows + 1 - w, 0)
        cnt = (rows + 1 - lo).astype(jnp.float32)[None, :, None]
        c0, c1 = g * POOL_GROUP, (g + 1) * POOL_GROUP
        outs.append((hi[:, :, c0:c1] - cs[:, lo, c0:c1]) / cnt)
    return jnp.concatenate(outs, axis=-1) - uf[:, n_prefix:]


def pool_branch(xn, u_prefix, w_in, pool_mix, pool_scale, w_out):
    b, t, _ = xn.shape
    u, gate = jnp.split(xn @ w_in, 2, axis=-1)
    u_ext = jnp.concatenate([u_prefix.astype(u.dtype), u], axis=1)
    p = multiscale_pool(u_ext, u_prefix.shape[1]).astype(xn.dtype)
    p = jnp.einsum('btgc,gcd->btgd', p.reshape(b, t, N_POOL_GROUPS, POOL_GROUP), pool_mix).reshape(b, t, POOL_WIDTH)
    o = p * pool_scale * jax.nn.silu(gate)
    return o @ w_out, u_ext[:, -POOL_STATE:]


def swa_attend(q, k, v, dist, valid, sinks, rel_bias):
    nq, nk = dist.shape
    bias = rel_bias.astype(jnp.float32)[t5_bucket(dist)]
    bias = jnp.transpose(bias, (2, 0, 1)).reshape(N_KV_HEADS, GROUP, nq, nk)
    logits = jnp.einsum('bnqkgd,bnskd->bnkgqs', q, k, preferred_element_type=jnp.float32) * SCALE + bias
    logits = jnp.where(valid[None, :, None, None], logits, NEG)
    sink = sinks.astype(jnp.float32).reshape(N_KV_HEADS, GROUP, 1, 1)
    m = jnp.maximum(jnp.max(logits, axis=-1, keepdims=True), sink)
    p = jnp.exp(logits - m)
    denom = jnp.sum(p, axis=-1, keepdims=True) + jnp.exp(sink - m)
    return jnp.einsum('bnkgqs,bnskd->bnqkgd', (p / denom).astype(v.dtype), v)


def swa_branch(xn, k_buf, v_buf, w_in, sinks, rel_bias, w_out):
    b, t, _ = xn.shape
    q, k, v, gate = jnp.split(xn @ w_in, [ATT_WIDTH, ATT_WIDTH + KV_WIDTH, ATT_WIDTH + 2 * KV_WIDTH], axis=-1)
    q = q.reshape(b, t, N_KV_HEADS, GROUP, HEAD_DIM)
    k = k.reshape(b, t, N_KV_HEADS, HEAD_DIM)
    v = v.reshape(b, t, N_KV_HEADS, HEAD_DIM)
    if k_buf is None:
        nb = t // BLOCK
        qb = q.reshape(b, nb, BLOCK, N_KV_HEADS, GROUP, HEAD_DIM)

        def band(a):
            ap = jnp.concatenate([jnp.zeros_like(a[:, :BLOCK]), a], axis=1).reshape(b, nb + 1, BLOCK, N_KV_HEADS, HEAD_DIM)
            return jnp.concatenate([ap[:, :-1], ap[:, 1:]], axis=2)

        qi = jnp.arange(BLOCK)[:, None]
        ki = jnp.arange(2 * BLOCK)[None, :]
        dist = qi + BLOCK - ki
        key_pos = jnp.arange(nb)[:, None, None] * BLOCK - BLOCK + ki[None]
        valid = (dist >= 0) & (dist < WINDOW) & (key_pos >= 0)
        o = swa_attend(qb, band(k), band(v), dist, valid, sinks, rel_bias)
        k_all, v_all = k, v
    else:
        k_all = jnp.concatenate([k_buf.astype(k.dtype), k], axis=1)
        v_all = jnp.concatenate([v_buf.astype(v.dtype), v], axis=1)
        dist = jnp.arange(t)[:, None] + WINDOW - jnp.arange(WINDOW + t)[None, :]
        valid = ((dist >= 0) & (dist < WINDOW))[None]
        o = swa_attend(q[:, None], k_all[:, None], v_all[:, None], dist, valid, sinks, rel_bias)
    o = o.reshape(b, t, ATT_WIDTH) * jax.nn.silu(gate)
    return o @ w_out, k_all[:, -WINDOW:], v_all[:, -WINDOW:]


def fox_prompt(q, k, v, logf):
    b, s = q.shape[:2]
    nb = s // BLOCK
    c = heads_first(jnp.cumsum(logf, axis=1))
    key_pos = jnp.arange(s)

    def one_block(i):
        q_i = lax.dynamic_slice_in_dim(q, i * BLOCK, BLOCK, axis=1)
        c_i = lax.dynamic_slice_in_dim(c, i * BLOCK, BLOCK, axis=3)
        logits = jnp.einsum('bqkgd,bskd->bkgqs', q_i, k, preferred_element_type=jnp.float32) * SCALE
        logits = logits + c_i[..., :, None] - c[..., None, :]
        qpos = i * BLOCK + jnp.arange(BLOCK)
        logits = jnp.where(key_pos[None, :] <= qpos[:, None], logits, NEG)
        p = jax.nn.softmax(logits, axis=-1)
        return jnp.einsum('bkgqs,bskd->bqkgd', p.astype(v.dtype), v)

    out = lax.map(one_block, jnp.arange(nb))
    return out.transpose(1, 0, 2, 3, 4, 5).reshape(b, s, ATT_WIDTH)


def fox_sample(q, k, v, logf, k_past, v_past, logf_past):
    b, t = q.shape[:2]
    n_past = k_past.shape[1]
    c_past = jnp.cumsum(logf_past.astype(jnp.float32), axis=1)
    c_new = c_past[:, -1:] + jnp.cumsum(logf, axis=1)
    cp, cn = heads_first(c_past), heads_first(c_new)
    lp = jnp.einsum('bqkgd,bskd->bkgqs', q, k_past.astype(q.dtype), preferred_element_type=jnp.float32) * SCALE
    lp = lp + cn[..., :, None] - cp[..., None, :]
    ln = jnp.einsum('bqkgd,bskd->bkgqs', q, k, preferred_element_type=jnp.float32) * SCALE
    ln = ln + cn[..., :, None] - cn[..., None, :]
    ln = jnp.where(jnp.tril(jnp.ones((t, t), bool)), ln, NEG)
    p = jax.nn.softmax(jnp.concatenate([lp, ln], axis=-1), axis=-1).astype(v.dtype)
    o = jnp.einsum('bkgqs,bskd->bqkgd', p[..., :n_past], v_past.astype(v.dtype)) + jnp.einsum('bkgqs,bskd->bqkgd', p[..., n_past:], v)
    return o.reshape(b, t, ATT_WIDTH)


def fox_branch(xn, past, w_in, f_bias, w_out):
    b, t, _ = xn.shape
    q, k, v, f, gate = jnp.split(xn @ w_in, [ATT_WIDTH, ATT_WIDTH + KV_WIDTH, ATT_WIDTH + 2 * KV_WIDTH, ATT_WIDTH + 2 * KV_WIDTH + N_HEADS], axis=-1)
    q = q.reshape(b, t, N_KV_HEADS, GROUP, HEAD_DIM)
    k = k.reshape(b, t, N_KV_HEADS, HEAD_DIM)
    v = v.reshape(b, t, N_KV_HEADS, HEAD_DIM)
    logf = jax.nn.log_sigmoid(f.astype(jnp.float32) + f_bias.astype(jnp.float32))
    if past is None:
        o = fox_prompt(q, k, v, logf)
    else:
        o = fox_sample(q, k, v, logf, *past)
    o = o * jax.nn.silu(gate)
    return o @ w_out, k, v, logf.astype(xn.dtype)


def setup_inputs(seed: int = 0) -> dict:
    key = jax.random.key(seed)
    ks = jax.random.split(key, 24)
    f32 = jnp.float32
    nrm = jax.random.normal
    n_pages = PAST_LEN // PAGE_SIZE
    n_used = DEC_BATCH * n_pages
    n_pool = (5 * n_used) // 4
    page_table = jax.random.permutation(ks[0], n_pool)[:n_used].reshape(DEC_BATCH, n_pages).astype(jnp.int32)
    return dict(
        x_prompt=nrm(ks[1], (BATCH, SEQ, D_MODEL), f32),
        x_sample=nrm(ks[2], (DEC_BATCH, DEC_SEQ, D_MODEL), f32),
        state_pool=nrm(ks[3], (N_A, DEC_BATCH, POOL_STATE, POOL_WIDTH), f32),
        cache_win_k=nrm(ks[4], (N_B, DEC_BATCH, WINDOW, N_KV_HEADS, HEAD_DIM), f32),
        cache_win_v=nrm(ks[5], (N_B, DEC_BATCH, WINDOW, N_KV_HEADS, HEAD_DIM), f32),
        cache_fox_k=nrm(ks[6], (N_C, n_pool, PAGE_SIZE, N_KV_HEADS, HEAD_DIM), f32),
        cache_fox_v=nrm(ks[7], (N_C, n_pool, PAGE_SIZE, N_KV_HEADS, HEAD_DIM), f32),
        cache_fox_logf=jax.nn.log_sigmoid(3.0 + nrm(ks[8], (N_C, n_pool, PAGE_SIZE, N_HEADS), f32)),
        page_table=page_table,
        norm_g=1.0 + 0.02 * nrm(ks[9], (DEPTH, D_MODEL), f32),
        final_norm_g=1.0 + 0.02 * nrm(ks[10], (D_MODEL,), f32),
        rel_bias=0.1 * nrm(ks[11], (N_BUCKETS, N_HEADS), f32),
        pool_w_in=nrm(ks[12], (N_A, D_MODEL, IN_A), f32) * D_MODEL ** -0.5,
        pool_mix=nrm(ks[13], (N_A, N_POOL_GROUPS, POOL_GROUP, POOL_GROUP), f32) * POOL_GROUP ** -0.5,
        pool_scale=1.0 + 0.1 * nrm(ks[14], (N_A, POOL_WIDTH), f32),
        pool_w_out=nrm(ks[15], (N_A, POOL_WIDTH, D_MODEL), f32) * POOL_WIDTH ** -0.5,
        swa_w_in=nrm(ks[16], (N_B, D_MODEL, IN_B), f32) * D_MODEL ** -0.5,
        swa_sinks=0.5 * nrm(ks[17], (N_B, N_HEADS), f32),
        swa_w_out=nrm(ks[18], (N_B, ATT_WIDTH, D_MODEL), f32) * ATT_WIDTH ** -0.5,
        fox_w_in=nrm(ks[19], (N_C, D_MODEL, IN_C), f32) * D_MODEL ** -0.5,
        fox_f_bias=jax.random.uniform(ks[20], (N_C, N_HEADS), f32, 1.0, 6.0),
        fox_w_out=nrm(ks[21], (N_C, ATT_WIDTH, D_MODEL), f32) * ATT_WIDTH ** -0.5,
    )


def reference(x_prompt, x_sample, state_pool, cache_win_k, cache_win_v, cache_fox_k, cache_fox_v, cache_fox_logf, page_table,
              norm_g, final_norm_g, rel_bias, pool_w_in, pool_mix, pool_scale, pool_w_out,
              swa_w_in, swa_sinks, swa_w_out, fox_w_in, fox_f_bias, fox_w_out):
    xp, xs = x_prompt, x_sample
    db = x_sample.shape[0]
    pool_p, pool_s = [], []
    wk_p, wv_p, wk_s, wv_s = [], [], [], []
    fk_p, fv_p, fl_p, fk_s, fv_s, fl_s = [], [], [], [], [], []
    for i in range(DEPTH):
        kind, j = i % N_MIXERS, i // N_MIXERS
        hp = rmsnorm(xp, norm_g[i])
        hs = rmsnorm(xs, norm_g[i])
        if kind == 0:
            w = (pool_w_in[j], pool_mix[j], pool_scale[j], pool_w_out[j])
            yp, st_p = pool_branch(hp, jnp.zeros((hp.shape[0], 0, POOL_WIDTH), hp.dtype), *w)
            ys, st_s = pool_branch(hs, state_pool[j], *w)
            pool_p.append(st_p)
            pool_s.append(st_s)
        elif kind == 1:
            w = (swa_w_in[j], swa_sinks[j], rel_bias, swa_w_out[j])
            yp, kp, vp = swa_branch(hp, None, None, *w)
            ys, ks_, vs_ = swa_branch(hs, cache_win_k[j], cache_win_v[j], *w)
            wk_p.append(kp)
            wv_p.append(vp)
            wk_s.append(ks_)
            wv_s.append(vs_)
        else:
            w = (fox_w_in[j], fox_f_bias[j], fox_w_out[j])
            past = (cache_fox_k[j, page_table].reshape(db, -1, N_KV_HEADS, HEAD_DIM),
                    cache_fox_v[j, page_table].reshape(db, -1, N_KV_HEADS, HEAD_DIM),
                    cache_fox_logf[j, page_table].reshape(db, -1, N_HEADS))
            yp, kp, vp, lp = fox_branch(hp, None, *w)
            ys, ks_, vs_, ls_ = fox_branch(hs, past, *w)
            fk_p.append(kp)
            fv_p.append(vp)
            fl_p.append(lp)
            fk_s.append(ks_)
            fv_s.append(vs_)
            fl_s.append(ls_)
        xp = xp + yp
        xs = xs + ys
    y_prompt = rmsnorm(xp, final_norm_g)
    y_sample = rmsnorm(xs, final_norm_g)
    return (y_prompt, y_sample, jnp.stack(pool_p), jnp.stack(pool_s), jnp.stack(wk_p), jnp.stack(wv_p), jnp.stack(wk_s), jnp.stack(wv_s), jnp.stack(fk_p), jnp.stack(fv_p), jnp.stack(fl_p), jnp.stack(fk_s), jnp.stack(fv_s), jnp.stack(fl_s))
```

```python
import os
import numpy as np
import concourse.bass as bass
import concourse.mybir as mybir
from concourse.bass_utils import run_bass_kernel_spmd

F32 = mybir.dt.float32
BF16 = mybir.dt.bfloat16
I32 = mybir.dt.int32
AF = mybir.ActivationFunctionType
ALU = mybir.AluOpType
AX = mybir.AxisListType

D = 1024
SEQ = 8192
TT = 512
NTILE = SEQ // TT
NS = 16
EPS = 1e-6


class Prog:
    def __init__(self, nc):
        self.nc = nc
        self.ops = []

    def op(self, eng, fn, R=(), W=(), dma=None, inc=16):
        W = tuple(W) + tuple(k for k in R if isinstance(k, str) and k.startswith('ps') and k[2:].isdigit() and k not in W)
        self.ops.append(dict(eng=eng, fn=fn, R=tuple(R), W=tuple(W), dma=dma, inc=inc))

    def emit(self, final_wait_all=True):
        nc = self.nc
        ops = self.ops
        lastw, readers = {}, {}
        lastchan = {}
        deps = [set() for _ in ops]
        for i, o in enumerate(ops):
            for k in o['R']:
                if k in lastw:
                    deps[i].add(lastw[k])
            for k in o['W']:
                if k in lastw:
                    deps[i].add(lastw[k])
                for r in readers.get(k, ()):
                    deps[i].add(r)
            if o['dma'] is not None:
                if o['dma'] in lastchan:
                    deps[i].add(lastchan[o['dma']])
                lastchan[o['dma']] = i
            for k in o['W']:
                lastw[k] = i
                readers[k] = []
            for k in o['R']:
                readers.setdefault(k, []).append(i)
            deps[i].discard(i)
            if o['eng'] == 'pe':
                deps[i] = {j for j in deps[i] if not (ops[j]['eng'] == 'pe' and ops[j]['dma'] is None)}
        if final_wait_all:
            last = {}
            for i, o in enumerate(ops):
                last[('d', o['dma']) if o['dma'] is not None else ('e', o['eng'])] = i
            self.ops.append(dict(eng='sp', fn=None, R=(), W=(), dma=None, inc=16))
            deps.append(set(last.values()))
        signal = [False] * len(ops)
        for i in range(len(ops)):
            for j in deps[i]:
                signal[j] = True
        semname, semval = [None] * len(ops), [0] * len(ops)
        cnt = {}
        for i, o in enumerate(ops):
            if o['dma'] is not None:
                key = 'dma:' + str(o['dma'])
                cnt[key] = cnt.get(key, 0) + o.get('inc', 16)
                semname[i], semval[i] = key, cnt[key]
            elif signal[i]:
                key = 'eng:' + o['eng']
                cnt[key] = cnt.get(key, 0) + 1
                semname[i], semval[i] = key, cnt[key]
        names = sorted(set(s for s in semname if s is not None))
        if os.environ.get('KDBG_PRINT'):
            print('SEMS', len(names), {k: v for k, v in cnt.items() if k.startswith('eng')}, max(cnt.values()), 'nops', len(ops))
        assert len(names) < 95, len(names)
        import contextlib
        stack = contextlib.ExitStack()
        sems = {n: stack.enter_context(nc.semaphore("s%d" % k)) for k, n in enumerate(names)}
        per_eng = {e: [] for e in ('pe', 'act', 'dve', 'pool', 'sp')}
        for i, o in enumerate(ops):
            per_eng[o['eng']].append(i)
        known = {e: {} for e in per_eng}

        def run_engine(ename, eng):
            kn = known[ename]
            for i in per_eng[ename]:
                o = ops[i]
                need = {}
                for j in deps[i]:
                    n, v = semname[j], semval[j]
                    if kn.get(n, 0) >= v:
                        continue
                    need[n] = max(need.get(n, 0), v)
                for n, v in need.items():
                    eng.wait_ge(sems[n], v)
                    kn[n] = v
                if o['fn'] is None:
                    continue
                ins = o['fn'](eng)
                if o['dma'] is not None:
                    ins.then_inc(sems[semname[i]], o.get('inc', 16))
                elif signal[i]:
                    ins.then_inc(sems[semname[i]], 1)

        with stack:
            with nc.Block() as block:
                @block.tensor
                def _(e):
                    run_engine('pe', e)

                @block.scalar
                def _(e):
                    run_engine('act', e)

                @block.vector
                def _(e):
                    run_engine('dve', e)

                @block.gpsimd
                def _(e):
                    run_engine('pool', e)

                @block.sync
                def _(e):
                    run_engine('sp', e)


NT = 5
NCOL = NT * TT
NEG = -30000.0


def build_program():
    import contextlib, os
    nc = bass.Bass("TRN2", target_bir_lowering=False)
    dt_in = lambda n, s, d=F32: nc.dram_tensor(n, s, d, kind="ExternalInput")
    dt_out = lambda n, s, d=F32: nc.dram_tensor(n, s, d, kind="ExternalOutput")
    dt_tmp = lambda n, s, d=F32: nc.dram_tensor(n, s, d)
    xT = dt_in("xT", [D, NCOL])
    gT = dt_in("gT", [128, 5, 8])
    pw_in = dt_in("pw_in", [2, D, 2048]); pmix = dt_in("pmix", [2, 4, 256, 256])
    pscale = dt_in("pscale", [128, 2, 8]); pw_out = dt_in("pw_out", [2, D, D])
    invc = dt_in("invc", [128, 4, 16])
    sw_in = dt_in("sw_in", [D, 2560]); sw_out = dt_in("sw_out", [D, D])
    sinks = dt_in("sinks", [1, 16]); relb = dt_in("relb", [32, 16])
    fw_in = dt_in("fw_in", [D, 2576]); fw_out = dt_in("fw_out", [D, D]); fbias = dt_in("fbias", [16, 1])
    cJ = dt_in("cJ", [128, 128]); cI = dt_in("cI", [128, 128]); cTri = dt_in("cTri", [128, 128])
    cOH = dt_in("cOH", [33, 384]); hmask_d = dt_in("hmask", [128, 1])
    idx_d = dt_in("idx", [128, 228], I32)
    yT = dt_out("yT", [D, 4 * TT]); pspT = dt_out("pspT", [2, D, 15])
    wkT = dt_out("wkT", [128, 4, 128]); wv = dt_out("wv", [128, 256])
    fkT = dt_out("fkT", [256, 4 * TT]); fv = dt_out("fv", [4 * TT, 256]); flT = dt_out("flT", [16, 4 * TT])
    xs_in = dt_in("xs_in", [NS, D]); stp = dt_in("stp", [2, NS, 15, D])
    cwk = dt_in("cwk", [NS, 128, 256]); cwv = dt_in("cwv", [NS, 128, 256])
    cfk = dt_in("cfk", [10240 * 128, 256]); cfv = dt_in("cfv", [10240 * 128, 256]); cfl = dt_in("cfl", [10240 * 128, 16])
    ptab = dt_in("ptab", [1, NS * 64], I32)
    cBD = dt_in("cBD", [16, 256]); cOHs = dt_in("cOHs", [33, 128])
    cTriU = dt_in("cTriU", [128, 128]); cIota = dt_in("cIota", [128, 1]); cNewB = dt_in("cNewB", [128, 16])
    ysT = dt_out("ysT", [128, 8, NS]); psS = dt_out("psS", [2, NS, 15, D])
    wkS = dt_out("wkS", [NS, 128, 256]); wvS = dt_out("wvS", [NS, 128, 256])
    fkS = dt_out("fkS", [NS, 256]); fvS = dt_out("fvS", [NS, 256]); flS = dt_out("flS", [NS, 16])
    osamp = dt_tmp("osamp", [NS, D])
    xres = dt_tmp("xres", [D, NCOL])
    tvd = dt_tmp("tvd", [16, 384], BF16)
    Gs = dt_tmp("Gs", [D, NCOL], BF16)
    send1 = dt_tmp("send1", [5120, TT], BF16); G1 = dt_tmp("G1", [4 * 5120, TT], BF16)
    sendV = dt_tmp("sendV", [4 * 2048, 64], BF16); GV = dt_tmp("GV", [4 * 8192, 64], BF16)
    sendL = dt_tmp("sendL", [16, 2048], F32); GL = dt_tmp("GL", [64, 2048], F32)
    send2 = dt_tmp("send2", [4096, TT], BF16); G2 = dt_tmp("G2", [4 * 4096, TT], BF16)
    GROUPS = [[0, 1, 2, 3], [4, 5, 6, 7]]
    P = Prog(nc)
    with contextlib.ExitStack() as st:
        sb = lambda n, s, d: st.enter_context(nc.sbuf_tensor(n, s, d))
        psb = [st.enter_context(nc.psum_tensor("ps%d" % i, [128, TT], F32)) for i in range(8)]
        pctr = [0]
        actr = [0]

        def nextps():
            i = pctr[0] % 4
            pctr[0] += 1
            return psb[i], 'ps%d' % i

        def nextacc():
            i = 6 + actr[0] % 2
            actr[0] += 1
            return psb[i], 'ps%d' % i

        Wa = sb("Wa", [128, 8 * 2320], BF16); W = Wa[:, :].rearrange("p (c n) -> p c n", c=8)
        WOa = sb("WOa", [128, 8 * D], BF16); WO = WOa[:, :].rearrange("p (c n) -> p c n", c=8)
        wkd = sb("wkd", [128, 8, 4, 128], BF16)
        w_mix = sb("w_mix", [128, 4, 2, 256], BF16)
        xt = sb("xt", [128, 8, TT], F32)
        xn = sb("xn", [128, 8, TT], BF16)
        rstd = sb("rstd", [128, TT], F32)
        A1 = sb("A1", [128, 8192], BF16); gate = A1[:, 0:4096].rearrange("p (c t) -> p c t", c=8)
        A2 = sb("A2", [128, 8192], BF16)
        sq = A2[:, 0:4096].rearrange("p (c t) -> p c t", c=8)
        pt = A2[:, 0:4096].rearrange("p (c t) -> p c t", c=8)
        ot = A2[:, 4096:8192].rearrange("p (c t) -> p c t", c=8)
        A3 = sb("A3", [128, 8192], BF16); qT = A3[:, 0:4096].rearrange("p (c t) -> p c t", c=8)
        kstage = A3[:, 4096:5120].rearrange("p (c t) -> p c t", c=2)
        PA = sb("PA", [128, 6336], F32)
        uext = PA[:, 0:4224].rearrange("p (c t) -> p c t", c=8)
        sA = PA[:, 4224:5280].rearrange("p (c t) -> p c t", c=2)
        sB = PA[:, 5280:6336].rearrange("p (c t) -> p c t", c=2)
        kdup = PA[:, 0:1280].bitcast(BF16).rearrange("p (a b) -> p a b", a=4)
        vsw = PA[:, 1280:3220].bitcast(BF16).rearrange("p (a b c) -> p a b c", a=5, b=4)
        Ct = PA[:, 3232:5280].bitcast(BF16).rearrange("p (a b) -> p a b", a=16)
        f32s = PA[:, 5280:5792]
        vst = PA[:, 5792:6048]
        kstage = PA[:, 0:512].bitcast(BF16).rearrange("p (c t) -> p c t", c=2)
        lgt = PA[0:16, 512:1024]
        f32f = PA[:, 1024:1536]
        vstf = PA[:, 1536:1792]
        vstb = PA[:, 1792:1920].bitcast(BF16)
        ones16 = PA[0:16, 1920:2432]
        lg = PA[:, 0:1024]
        cq = PA[:, 1024:2048]
        onesq = PA[:, 2048:3072]
        Qp = [PA[:, 3072 + 256 * i:3328 + 256 * i].bitcast(BF16) for i in range(2)]
        negct = PA[:, 3584:3840].rearrange("p (a b) -> p a b", a=64)
        ostage = [PA[0:64, 3840 + 256 * i:4096 + 256 * i].bitcast(BF16) for i in range(2)]
        ptb = [sb("ptb%d" % i, [128, TT], BF16) for i in range(3)]
        ptc = [0]
        rr = sb("rr", [128, TT], F32)
        bcs = sb("bcs", [128, TT], F32)
        t1 = sb("t1", [128, TT], F32)
        ones = sb("ones", [128, 128], BF16); ones32 = sb("ones32", [128, 128], F32)
        Jb = sb("Jb", [128, 128], BF16); Ib = sb("Ib", [128, 128], BF16); I32t = sb("I32t", [128, 128], F32)
        trib = sb("trib", [128, 128], BF16)
        gt = sb("gt", [128, 5, 8], F32); psc = sb("psc", [128, 2, 8], F32); invct = sb("invct", [128, 4, 16], F32)
        es = sb("es", [128, 16], F32)
        rbx = sb("rbx", [33, 16], F32); oht = sb("oht", [33, 384], F32); tvs = sb("tvs", [16, 384], BF16)
        hmask = sb("hmaskt", [128, 1], F32); negfb = sb("negfb", [16, 1], F32)
        idx = sb("idxt", [128, 228], I32)
        fdummy = sb("fdummy", [1, 8], F32)
        PAKEYS = ['FAu', 'FAa', 'FAb', 'kdup', 'vsw', 'Ct', 'f32s', 'vst', 'kstage', 'lgt', 'f32f', 'vstf', 'vstb', 'ones16',
                  'lg', 'cq', 'onesq', 'Qp0', 'Qp1', 'negct', 'ostage0', 'ostage1']

        def fence():
            P.op('dve', lambda e: e.memset(fdummy[:, :], 0.0), R=PAKEYS, W=PAKEYS + ['fdummy'])

        dve = lambda fn, R, W_: P.op('dve', fn, R=R, W=W_)
        act = lambda fn, R, W_: P.op('act', fn, R=R, W=W_)
        pe = lambda fn, R, W_: P.op('pe', fn, R=R, W=W_)
        spd = lambda fn, R, W_, ch: P.op('sp', fn, R=R, W=W_, dma=ch)
        pld = lambda fn, R, W_, ch: P.op('pool', fn, R=R, W=W_, dma=ch)

        dve(lambda e: e.memset(ones[:, :], 1.0), [], ['ones'])
        dve(lambda e: e.memset(ones32[:, :], 1.0), [], ['ones32'])
        spd(lambda e: e.dma_start(out=gt[:, :, :], in_=gT.ap()), [], ['gt'], 'setup')
        spd(lambda e: e.dma_start(out=psc[:, :, :], in_=pscale.ap()), [], ['psc'], 'setup')
        spd(lambda e: e.dma_start(out=invct[:, :, :], in_=invc.ap()), [], ['invct'], 'setup')
        spd(lambda e: e.dma_start(out=hmask[:, :], in_=hmask_d.ap()), [], ['hmask'], 'setup')
        spd(lambda e: e.dma_start(out=idx[:, :], in_=idx_d.ap()), [], ['idx'], 'setup')
        spd(lambda e: e.dma_start(out=I32t[:, :], in_=cI.ap()), [], ['I32t'], 'setup')
        spd(lambda e: e.dma_start(out=negfb[:, :], in_=fbias.ap()), [], ['negfb'], 'setup')
        dve(lambda e: e.tensor_scalar(out=negfb[:, :], in0=negfb[:, :], scalar1=-1.0, scalar2=None, op0=ALU.mult), ['negfb'], ['negfb'])
        pld(lambda e: e.dma_start(out=Jb[:, :], in_=cJ.ap()), [], ['Jb'], 'setupp')
        pld(lambda e: e.dma_start(out=Ib[:, :], in_=cI.ap()), [], ['Ib'], 'setupp')
        pld(lambda e: e.dma_start(out=trib[:, :], in_=cTri.ap()), [], ['trib'], 'setupp')
        spd(lambda e: e.dma_start(out=es[:, :], in_=sinks.ap().partition_broadcast(128)), [], ['es'], 'setup')
        act(lambda e: e.activation(out=es[:, :], in_=es[:, :], func=AF.Exp), ['es'], ['es'])
        xres_v = xres.ap().rearrange("(c p) t -> p c t", p=128)
        xT_v = xT.ap().rearrange("(c p) t -> p c t", p=128)
        yT_v = yT.ap().rearrange("(c p) t -> p c t", p=128)
        Gs_v = Gs.ap().rearrange("(c p) t -> p c t", p=128)

        def load_w(dst, key, src_ap, c0, ncol, d0):
            for c in range(8):
                pld(lambda e, c=c: e.dma_start(out=dst[:, c, d0:d0 + ncol], in_=src_ap[c * 128:(c + 1) * 128, c0:c0 + ncol]), [], [key] + (['KPG0', 'KPG1', 'VPG0', 'VPG1', 'KTt', 'Lg', 'Lhp', 'Sx'] if key == 'W' else []), key + str(c % 2))

        def rmsnorm(layer, inplace=False):
            act(lambda e: e.activation(out=sq[:, :, :], in_=xt[:, :, :], func=AF.Square), ['xt'], ['A2'])
            pp, pk = nextps()
            for c in range(8):
                pe(lambda e, c=c: e.matmul(pp[:, :], lhsT=ones[:, :], rhs=sq[:, c, :], start=(c == 0), stop=(c == 7)), ['ones', 'A2'], [pk])
            dve(lambda e: e.tensor_scalar(out=rstd[:, :], in0=pp[:, :], scalar1=1.0 / D, scalar2=EPS, op0=ALU.mult, op1=ALU.add), [pk], ['rstd'])
            act(lambda e: e.activation(out=rstd[:, :], in_=rstd[:, :], func=AF.Ln), ['rstd'], ['rstd'])
            act(lambda e: e.activation(out=rstd[:, :], in_=rstd[:, :], func=AF.Exp, scale=-0.5), ['rstd'], ['rstd'])
            for c in range(8):
                if inplace:
                    dve(lambda e, c=c: e.scalar_tensor_tensor(out=xt[:, c, :], in0=xt[:, c, :], scalar=gt[:, layer, c:c + 1], in1=rstd[:, :], op0=ALU.mult, op1=ALU.mult), ['xt', 'gt', 'rstd'], ['xt'])
                else:
                    dve(lambda e, c=c: e.scalar_tensor_tensor(out=xn[:, c, :], in0=xt[:, c, :], scalar=gt[:, layer, c:c + 1], in1=rstd[:, :], op0=ALU.mult, op1=ALU.mult), ['xt', 'gt', 'rstd'], ['xn'])

        def proj(wc0, nchunk, evac, wt=None, wkey='W'):
            for m in range(nchunk):
                pp, pk = nextps()
                for k in range(8):
                    if wt is None:
                        l = W[:, k, wc0 + m * 128:wc0 + (m + 1) * 128]
                    else:
                        l = wt(k, m)
                    pe(lambda e, l=l, k=k, pp=pp: e.matmul(pp[:, :], lhsT=l, rhs=xn[:, k, :], start=(k == 0), stop=(k == 7)), [wkey, 'xn'], [pk])
                evac(m, pp, pk)

        def out_proj_residual():
            for c in range(8):
                pp, pk = nextps()
                for k in range(8):
                    pe(lambda e, c=c, k=k, pp=pp: e.matmul(pp[:, :], lhsT=WO[:, k, c * 128:(c + 1) * 128], rhs=ot[:, k, :], start=(k == 0), stop=(k == 7)), ['WO', 'A2'], [pk])
                dve(lambda e, c=c, pp=pp: e.tensor_tensor(out=xt[:, c, :], in0=pp[:, :], in1=xt[:, c, :], op=ALU.add), [pk, 'xt'], ['xt'])

        def load_x(src_v, t, first):
            spd(lambda e: e.dma_start(out=xt[:, :, :], in_=src_v[:, :, t * TT:(t + 1) * TT]), [] if first else ['xres%d' % t], ['xt'], 'xt')

        def store_x(t):
            spd(lambda e: e.dma_start(out=xres_v[:, :, t * TT:(t + 1) * TT], in_=xt[:, :, :]), ['xt'], ['xres%d' % t], 'xt')

        def pool_layer(layer, j, first, final):
            fence()
            load_w(W, 'W', pw_in[j], 0, 2048, 0)
            load_w(WO, 'WO', pw_out[j], 0, D, 0)
            for g in range(4):
                for kc in range(2):
                    pld(lambda e, g=g, kc=kc: e.dma_start(out=w_mix[:, g, kc, :], in_=pmix[j, g, kc * 128:(kc + 1) * 128, :]), [], ['w_mix'], 'w_mix%d' % kc)
            dve(lambda e: e.memset(uext[:, :, 0:16], 0.0), [], ['FAu'])
            for t in range(NT):
                load_x(xT_v if first else xres_v, t, first)
                rmsnorm(layer)
                proj(0, 8, lambda m, pp, pk: act(lambda e: e.activation(out=uext[:, m, 16:528], in_=pp[:, :], func=AF.Copy), [pk], ['FAu']))
                proj(D, 8, lambda m, pp, pk: act(lambda e: e.activation(out=gate[:, m, :], in_=pp[:, :], func=AF.Silu), [pk], ['A1']))
                for g in range(4):
                    w = 2 << g
                    A = uext[:, 2 * g:2 * g + 2, :]
                    cur, curk = A, 'FAu'
                    bufs = [(sA, 'FAa'), (sB, 'FAb')]
                    k, bi = 1, 0
                    while k < w:
                        nb, nk = bufs[bi]
                        lo = 2 * k - 1
                        dve(lambda e, cur=cur, nb=nb, lo=lo, k=k: e.tensor_tensor(out=nb[:, :, lo:528], in0=cur[:, :, lo:528], in1=cur[:, :, lo - k:528 - k], op=ALU.add), [curk], [nk])
                        cur, curk = nb, nk
                        k *= 2
                        bi ^= 1
                    dve(lambda e, cur=cur, A=A, g=g, w=w: e.scalar_tensor_tensor(out=pt[:, 2 * g:2 * g + 2, :], in0=cur[:, :, 16:528], scalar=1.0 / w, in1=A[:, :, 16:528], op0=ALU.mult, op1=ALU.subtract), [curk, 'FAu'], ['A2'])
                    if t == 1:
                        for cc in range(2):
                            dve(lambda e, cur=cur, g=g, cc=cc: e.tensor_tensor(out=t1[:, 0:16], in0=cur[:, cc, 16:32], in1=invct[:, g, :], op=ALU.mult), [curk, 'invct'], ['t1'])
                            dve(lambda e, A=A, g=g, cc=cc: e.tensor_tensor(out=pt[:, 2 * g + cc, 0:16], in0=t1[:, 0:16], in1=A[:, cc, 16:32], op=ALU.subtract), ['t1', 'FAu'], ['A2'])
                if t == NT - 1:
                    spd(lambda e: e.dma_start(out=pspT[j].rearrange("(c p) s -> p c s", p=128), in_=uext[:, :, 513:528]), ['FAu'], [], 'psp%d' % j)
                dve(lambda e: e.tensor_copy(out=uext[:, :, 0:16], in_=uext[:, :, 512:528]), ['FAu', 'FAa', 'FAb'], ['FAu'])
                for c in range(8):
                    g, half = c // 2, c % 2
                    pp, pk = nextps()
                    for kc in range(2):
                        pe(lambda e, g=g, half=half, kc=kc, pp=pp: e.matmul(pp[:, :], lhsT=w_mix[:, g, kc, half * 128:(half + 1) * 128], rhs=pt[:, 2 * g + kc, :], start=(kc == 0), stop=(kc == 1)), ['w_mix', 'A2'], [pk])
                    dve(lambda e, c=c, pp=pp: e.scalar_tensor_tensor(out=ot[:, c, :], in0=pp[:, :], scalar=psc[:, j, c:c + 1], in1=gate[:, c, :], op0=ALU.mult, op1=ALU.mult), [pk, 'psc', 'A1'], ['A2o'])
                for c in range(8):
                    pp, pk = nextps()
                    for k in range(8):
                        pe(lambda e, c=c, k=k, pp=pp: e.matmul(pp[:, :], lhsT=WO[:, k, c * 128:(c + 1) * 128], rhs=ot[:, k, :], start=(k == 0), stop=(k == 7)), ['WO', 'A2o'], [pk])
                    dve(lambda e, c=c, pp=pp: e.tensor_tensor(out=xt[:, c, :], in0=pp[:, :], in1=xt[:, c, :], op=ALU.add), [pk, 'xt'], ['xt'])
                if final:
                    if t >= 1:
                        rmsnorm(4, inplace=True)
                        spd(lambda e, t=t: e.dma_start(out=yT_v[:, :, (t - 1) * TT:t * TT], in_=xt[:, :, :]), ['xt'], [], 'xt')
                else:
                    store_x(t)

        def swa_layer(layer):
            fence()
            dve(lambda e: e.memset(rbx[:, :], NEG), [], ['rbx'])
            spd(lambda e: e.dma_start(out=rbx[0:32, :], in_=relb.ap()), [], ['rbx'], 'setup')
            spd(lambda e: e.dma_start(out=oht[:, :], in_=cOH.ap()), [], ['oht'], 'setup')
            pp0, pk0 = nextps()
            pe(lambda e: e.matmul(pp0[0:16, 0:384], lhsT=rbx[:, :], rhs=oht[:, :], start=True, stop=True), ['rbx', 'oht'], [pk0])
            act(lambda e: e.activation(out=tvs[:, :], in_=pp0[0:16, 0:384], func=AF.Copy), [pk0], ['tvs'])
            spd(lambda e: e.dma_start(out=tvd.ap(), in_=tvs[:, :]), ['tvs'], ['tvd'], 'tvs')
            spd(lambda e: e.dma_start(out=Ct[:, :, :], in_=bass.AP(tvd.ap().tensor, 0, [[1, 128], [384, 16], [1, 256]])), ['tvd'], ['Ct'], 'Ct')
            dve(lambda e: e.memset(vsw[:, :, :, :], 0.0), [], ['vsw'])
            dve(lambda e: e.memset(vsw[:, :, :, 64:68:2], 1.0), [], ['vsw'])
            dve(lambda e: e.memset(kdup[:, :, :], 0.0), [], ['kdup'])

            load_w(W, 'W', sw_in.ap(), 0, 1024, 0)
            load_w(W, 'W', sw_in.ap(), 1280, 256, 1024)
            load_w(W, 'W', sw_in.ap(), 1536, 1024, 1280)
            load_w(WO, 'WO', sw_out.ap(), 0, D, 0)
            for kv in range(4):
                for half in range(2):
                    for c in range(8):
                        pld(lambda e, kv=kv, half=half, c=c: e.dma_start(out=wkd[:, c, kv, half * 64:(half + 1) * 64], in_=sw_in[c * 128:(c + 1) * 128, 1024 + kv * 64:1024 + (kv + 1) * 64]), [], ['wkd'], 'wkd%d' % (c % 2))
            for t in (range(NT) if int(os.environ.get('KDBG_SWA', '3')) >= 1 else []):
                load_x(xres_v, t, False)
                rmsnorm(layer)
                proj(0, 8, lambda m, pp, pk: act(lambda e: e.activation(out=qT[:, m, :], in_=pp[:, :], func=AF.Copy, scale=0.125), [pk], ['A3']))

                def evk(m, pp, pk):
                    act(lambda e: e.activation(out=kdup[:, m, 128:640], in_=pp[:, :], func=AF.Copy), [pk], ['kdup'])
                    if t == NT - 1:
                        act(lambda e: e.activation(out=f32s[:, 0:128], in_=pp[:, 384:512], func=AF.Copy), [pk], ['f32s'])
                        spd(lambda e: e.dma_start(out=wkT[:, m, :], in_=f32s[:, 0:128]), ['f32s'], [], 'f32s')
                CUT = int(os.environ.get('KDBG_CUT', '0'))
                if not CUT & 2:
                    proj(0, 4, evk, wt=lambda k, m: wkd[:, k, m, :], wkey='wkd')
                for blk in (range(4) if not CUT & 4 else []):
                    pp, pk = nextps()
                    for k in range(8):
                        pe(lambda e, k=k, blk=blk, pp=pp: e.matmul(pp[:, 0:256], lhsT=xn[:, k, blk * 128:(blk + 1) * 128], rhs=W[:, k, 1024:1280], start=(k == 0), stop=(k == 7)), ['W', 'xn'], [pk])
                    if t == NT - 1 and blk == 3 and not CUT & 32:
                        act(lambda e, pp=pp: e.activation(out=vst[:, :], in_=pp[:, 0:256], func=AF.Copy), [pk], ['vst'])
                        spd(lambda e: e.dma_start(out=wv.ap(), in_=vst[:, :]), ['vst'], [], 'vst')
                    for kv in (range(4) if not CUT & 64 else []):
                        act(lambda e, blk=blk, kv=kv, pp=pp: e.activation(out=vsw[:, blk + 1, kv, 0:64], in_=pp[:, kv * 64:(kv + 1) * 64], func=AF.Copy), [pk], ['vsw'])
                        dve(lambda e, blk=blk, kv=kv, pp=pp: e.tensor_copy(out=vsw[:, blk + 1, kv, 130:194], in_=pp[:, kv * 64:(kv + 1) * 64]), [pk], ['vsw'])
                proj(1280, 8, lambda m, pp, pk: act(lambda e: e.activation(out=gate[:, m, :], in_=pp[:, :], func=AF.Silu), [pk], ['A1']))
                SW = int(os.environ.get('KDBG_SWA', '3'))
                if SW < 3:
                    dve(lambda e: e.tensor_copy(out=ot[:, :, :], in_=gate[:, :, :]), ['A1'], ['A2o'])
                for qb in (range(4) if SW >= 2 else []):
                    for kv in range(4):
                        co = 256 * (actr[0] % 2)
                        actr[0] += 1
                        accE, akE, accO, akO = psb[6], 'ps6', psb[7], 'ps7'
                        kbs = [0, 1]
                        if t == 0 and qb == 0:
                            kbs = [1]
                        for ki, kb in enumerate(kbs):
                            bi = qb + kb
                            off = 128 if kb == 0 else 0
                            pp, pk = nextps()
                            for par in range(2):
                                r0 = par * 64
                                oc = pp[:, par * 256:(par + 1) * 256].rearrange("p (a b) -> p a b", a=2)
                                pe(lambda e, r0=r0, oc=oc, bi=bi, kv=kv, qb=qb: e.matmul(oc, lhsT=kdup[r0:r0 + 64, kv, bi * 128:(bi + 1) * 128], rhs=qT[r0:r0 + 64, 2 * kv:2 * kv + 2, qb * 128:(qb + 1) * 128], start=True, stop=False), ['kdup', 'A3'], [pk])
                                pe(lambda e, oc=oc, par=par, kv=kv, off=off: e.matmul(oc, lhsT=Jb[:, :], rhs=Ct[:, 4 * kv + par:4 * kv + 4:2, off:off + 128], start=False, stop=True), ['Jb', 'Ct'], [pk])
                            pb, pbk = ptb[ptc[0] % 3], 'ptb%d' % (ptc[0] % 3)
                            ptc[0] += 1
                            if t == 1 and qb == 0 and kb == 0:
                                act(lambda e, pb=pb, pp=pp: e.activation(out=pb[:, :], in_=pp[:, :], func=AF.Exp, bias=hmask[:, 0:1]), [pk, 'hmask'], [pbk])
                            else:
                                act(lambda e, pb=pb, pp=pp: e.activation(out=pb[:, :], in_=pp[:, :], func=AF.Exp), [pk], [pbk])
                            st_, sp_ = (ki == 0), (ki == len(kbs) - 1)
                            pe(lambda e, accE=accE, co=co, pb=pb, bi=bi, kv=kv, st_=st_, sp_=sp_: e.matmul(accE[0:65, co:co + 256], lhsT=vsw[:, bi, kv, 0:65], rhs=pb[:, 0:256], start=st_, stop=sp_), ['vsw', pbk], [akE])
                            pe(lambda e, accO=accO, co=co, pb=pb, bi=bi, kv=kv, st_=st_, sp_=sp_: e.matmul(accO[:, co:co + 256], lhsT=vsw[:, bi, kv, 66:194], rhs=pb[:, 256:512], start=st_, stop=sp_), ['vsw', pbk], [akO])
                        if SW < 3:
                            continue
                        dve(lambda e, accE=accE, co=co, kv=kv: e.tensor_tensor(out=rr[64:65, 0:256].rearrange("p (a b) -> p a b", a=2), in0=accE[64:65, co:co + 256].rearrange("p (a b) -> p a b", a=2), in1=es[64:65, 4 * kv:4 * kv + 4:2].unsqueeze(2).to_broadcast([1, 2, 128]), op=ALU.add), [akE, 'es'], ['rr'])
                        dve(lambda e, accO=accO, co=co, kv=kv: e.tensor_tensor(out=rr[0:1, 256:512].rearrange("p (a b) -> p a b", a=2), in0=accO[0:1, co:co + 256].rearrange("p (a b) -> p a b", a=2), in1=es[0:1, 4 * kv + 1:4 * kv + 4:2].unsqueeze(2).to_broadcast([1, 2, 128]), op=ALU.add), [akO, 'es'], ['rr'])
                        dve(lambda e: e.reciprocal(out=rr[64:65, 0:256], in_=rr[64:65, 0:256]), ['rr'], ['rr'])
                        dve(lambda e: e.reciprocal(out=rr[0:1, 256:512], in_=rr[0:1, 256:512]), ['rr'], ['rr'])
                        pb2, pb2k = nextps()
                        pe(lambda e, pb2=pb2: e.matmul(pb2[0:64, 0:256], lhsT=ones32[64:65, 0:64], rhs=rr[64:65, 0:256], start=True, stop=True), ['ones32', 'rr'], [pb2k])
                        pe(lambda e, pb2=pb2: e.matmul(pb2[:, 256:512], lhsT=ones32[0:1, :], rhs=rr[0:1, 256:512], start=True, stop=True), ['ones32', 'rr'], [pb2k])
                        act(lambda e, pb2=pb2: e.activation(out=bcs[0:64, 0:256], in_=pb2[0:64, 0:256], func=AF.Copy), [pb2k], ['bcs'])
                        act(lambda e, pb2=pb2: e.activation(out=bcs[64:128, 256:512], in_=pb2[64:128, 256:512], func=AF.Copy), [pb2k], ['bcs'])
                        dve(lambda e, accE=accE, co=co: e.tensor_tensor(out=t1[0:64, 0:256], in0=accE[0:64, co:co + 256], in1=bcs[0:64, 0:256], op=ALU.mult), [akE, 'bcs'], ['t1'])
                        dve(lambda e, accO=accO, co=co: e.tensor_tensor(out=t1[64:128, 256:512], in0=accO[64:128, co:co + 256], in1=bcs[64:128, 256:512], op=ALU.mult), [akO, 'bcs'], ['t1'])
                        dve(lambda e, kv=kv, qb=qb: e.tensor_tensor(out=ot[0:64, 2 * kv:2 * kv + 2, qb * 128:(qb + 1) * 128], in0=t1[0:64, 0:256].rearrange("p (a b) -> p a b", a=2), in1=gate[0:64, 2 * kv:2 * kv + 2, qb * 128:(qb + 1) * 128], op=ALU.mult), ['t1', 'A1'], ['A2o'])
                        dve(lambda e, kv=kv, qb=qb: e.tensor_tensor(out=ot[64:128, 2 * kv:2 * kv + 2, qb * 128:(qb + 1) * 128], in0=t1[64:128, 256:512].rearrange("p (a b) -> p a b", a=2), in1=gate[64:128, 2 * kv:2 * kv + 2, qb * 128:(qb + 1) * 128], op=ALU.mult), ['t1', 'A1'], ['A2o'])
                if not CUT & 16:
                  dve(lambda e: e.tensor_copy(out=kdup[:, :, 0:128], in_=kdup[:, :, 512:640]), ['kdup'], ['kdup'])
                if not CUT & 16:
                  dve(lambda e: e.tensor_copy(out=vsw[:, 0, :, 0:64], in_=vsw[:, 4, :, 0:64]), ['vsw'], ['vsw'])
                  dve(lambda e: e.tensor_copy(out=vsw[:, 0, :, 130:194], in_=vsw[:, 4, :, 130:194]), ['vsw'], ['vsw'])
                for c in range(8):
                    pp, pk = nextps()
                    for k in range(8):
                        pe(lambda e, c=c, k=k, pp=pp: e.matmul(pp[:, :], lhsT=WO[:, k, c * 128:(c + 1) * 128], rhs=ot[:, k, :], start=(k == 0), stop=(k == 7)), ['WO', 'A2o'], [pk])
                    dve(lambda e, c=c, pp=pp: e.tensor_tensor(out=xt[:, c, :], in0=pp[:, :], in1=xt[:, c, :], op=ALU.add), [pk, 'xt'], ['xt'])
                store_x(t)


        IQ, IK, IV, IL, IO = 0, 64, 80, 144, 152
        send1v = send1.ap().rearrange("(f tt) n -> f tt n", tt=4)
        send2v = send2.ap().rearrange("(f t) n -> f t n", t=16)
        sendVv = sendV.ap().rearrange("(kv tok) d -> tok kv d", kv=4)
        GLv = GL.ap().rearrange("r (two n) -> (r two) n", two=2)
        cprev = sb("cprev", [4, 2], F32)

        def gather(out_ap, src_ap, col, R, W_, ch):
            P.op('pool', lambda e: e.indirect_dma_start(out=out_ap, out_offset=None, in_=src_ap, in_offset=bass.IndirectOffsetOnAxis(ap=idx[:, col:col + 1], axis=0)), R=list(R) + ['idx'], W=W_, dma=ch)

        def allgather(src, dst, R, W_, ch, nchunk=1):
            rows = src.ap().shape[0] // nchunk
            for k in range(nchunk):
                P.op('pool', lambda e, k=k: e.collective_compute("AllGather", ALU.bypass, replica_groups=GROUPS, ins=[src[k * rows:(k + 1) * rows, :].opt()], outs=[dst[4 * k * rows:4 * (k + 1) * rows, :].opt()]), R=R, W=W_, dma=ch + str(k), inc=1)

        def fox_layer(layer):
            fence()
            load_w(W, 'W', fw_in.ap(), 0, 1024, 0)
            load_w(W, 'W', fw_in.ap(), 1280, 256, 1024)
            load_w(W, 'W', fw_in.ap(), 1536, 16, 1280)
            load_w(W, 'W', fw_in.ap(), 1552, 1024, 1296)
            for c in range(8):
                pld(lambda e, c=c: e.dma_start(out=wkd[:, c, 0:2, :], in_=fw_in[c * 128:(c + 1) * 128, 1024:1280].rearrange("p (a b) -> p a b", a=2)), [], ['wkd'], 'wkd%d' % (c % 2))
            load_w(WO, 'WO', fw_out.ap(), 0, D, 0)
            dve(lambda e: e.memset(ones16[:, :], 1.0), [], ['ones16'])
            s1keys, sVkeys, sLkeys = [], [], []
            for t in range(NT):
                load_x(xres_v, t, False)
                rmsnorm(layer)
                proj(1296, 8, lambda m, pp, pk: act(lambda e: e.activation(out=gate[:, m, :], in_=pp[:, :], func=AF.Silu), [pk], ['A1']))
                spd(lambda e, t=t: e.dma_start(out=Gs_v[:, :, t * TT:(t + 1) * TT], in_=gate[:, :, :]), ['A1'], ['Gs%d' % t], 'A1')
                if t == 0:
                    continue
                tt = t - 1
                FC = int(os.environ.get('KDBG_FCUT', '0'))
                if not FC & 8:
                  proj(0, 8, lambda m, pp, pk: act(lambda e: e.activation(out=qT[:, m, :], in_=pp[:, :], func=AF.Copy, scale=0.125), [pk], ['A3']))
                if not FC & 8:
                  spd(lambda e, tt=tt: e.dma_start(out=send1v[0:1024, tt, :].rearrange("(c p) n -> p c n", p=128), in_=qT[:, :, :]), ['A3'], ['s1q%d' % tt], 'A3')
                s1keys.append('s1q%d' % tt)

                def evk(m, pp, pk):
                    act(lambda e: e.activation(out=kstage[:, m, :], in_=pp[:, :], func=AF.Copy), [pk], ['kstage'])
                    dve(lambda e: e.tensor_copy(out=f32f[:, :], in_=pp[:, :]), [pk], ['f32f'])
                    spd(lambda e, m=m, tt=tt: e.dma_start(out=fkT[m * 128:(m + 1) * 128, tt * TT:(tt + 1) * TT], in_=f32f[:, :]), ['f32f'], [], 'f32f')
                if not FC & 4:
                  proj(0, 2, evk, wt=lambda k, m: wkd[:, k, m, :], wkey='wkd')
                if not FC & 4:
                  spd(lambda e, tt=tt: e.dma_start(out=send1v[1024:1280, tt, :].rearrange("(c p) n -> p c n", p=128), in_=kstage[:, :, :]), ['kstage'], ['s1k%d' % tt], 'kstage')
                s1keys.append('s1k%d' % tt)
                for blk in (range(4) if not FC & 2 else []):
                    pp, pk = nextps()
                    for k in range(8):
                        pe(lambda e, k=k, blk=blk, pp=pp: e.matmul(pp[:, 0:256], lhsT=xn[:, k, blk * 128:(blk + 1) * 128], rhs=W[:, k, 1024:1280], start=(k == 0), stop=(k == 7)), ['W', 'xn'], [pk])
                    act(lambda e, pp=pp: e.activation(out=vstf[:, :], in_=pp[:, 0:256], func=AF.Copy), [pk], ['vstf'])
                    dve(lambda e, pp=pp: e.tensor_copy(out=vstb[:, :], in_=pp[:, 0:256]), [pk], ['vstb'])
                    r0 = tt * TT + blk * 128
                    spd(lambda e, r0=r0: e.dma_start(out=fv[r0:r0 + 128, :], in_=vstf[:, :]), ['vstf'], [], 'vstf')
                    spd(lambda e, r0=r0: e.dma_start(out=sendVv[r0:r0 + 128, :, :], in_=vstb[:, :].rearrange("p (a b) -> p a b", a=4)), ['vstb'], ['sV%d' % r0], 'vstb')
                    sVkeys.append('sV%d' % r0)
                if FC & 1:
                    continue
                pp, pk = nextps()
                for k in range(8):
                    pe(lambda e, k=k, pp=pp: e.matmul(pp[0:16, :], lhsT=W[:, k, 1280:1296], rhs=xn[:, k, :], start=(k == 0), stop=(k == 7)), ['W', 'xn'], [pk])
                act(lambda e, pp=pp: e.activation(out=lgt[:, :], in_=pp[0:16, :], func=AF.Exp, bias=negfb[:, 0:1], scale=-1.0), [pk, 'negfb'], ['lgt'])
                dve(lambda e: e.tensor_scalar(out=lgt[:, :], in0=lgt[:, :], scalar1=1.0, scalar2=None, op0=ALU.add), ['lgt'], ['lgt'])
                act(lambda e: e.activation(out=lgt[:, :], in_=lgt[:, :], func=AF.Ln), ['lgt'], ['lgt'])
                dve(lambda e: e.tensor_scalar(out=lgt[:, :], in0=lgt[:, :], scalar1=-1.0, scalar2=None, op0=ALU.mult), ['lgt'], ['lgt'])
                spd(lambda e, tt=tt: e.dma_start(out=flT[:, tt * TT:(tt + 1) * TT], in_=lgt[:, :]), ['lgt'], [], 'lgt')
                spd(lambda e, tt=tt: e.dma_start(out=sendL[:, tt * TT:(tt + 1) * TT], in_=lgt[:, :]), ['lgt'], ['sL%d' % tt], 'lgt2')
                sLkeys.append('sL%d' % tt)
            SAMP = int(os.environ.get('KDBG_SAMP', '1'))
            if SAMP:
                s_fox_pre(layer)
            FX = int(os.environ.get('KDBG_FOX', '9'))
            if FX < 2:
                return
            allgather(send1, G1, s1keys, ['G1'], 'cc1', 5)
            allgather(sendV, GV, sVkeys, ['GV'], 'ccV')
            allgather(sendL, GL, sLkeys, ['GL'], 'ccL')
            if FX < 3:
                return
            fence()
            Kp = A2[:, :]
            Vp = A3[:, 0:4160].rearrange("p (a b) -> p a b", a=64)
            cr16 = A1[:, :]
            for i in range(16):
                gather(Kp[:, i * TT:(i + 1) * TT], G1.ap(), IK + i, ['G1'], ['A2', 'A2o'], 'gk%d' % (i % 2))
            dve(lambda e: e.memset(Kp[64:65, :], 1.0), [], ['A2', 'A2o'])
            dve(lambda e: e.memset(Vp[:, :, 64:65], 1.0), [], ['A3'])
            for blk in range(64):
                gather(Vp[:, blk, 0:64], GV.ap(), IV + blk, ['GV'], ['A3'], 'gv%d' % (blk % 2))
            dve(lambda e: e.memset(onesq[:, :], 1.0), [], ['onesq'])
            for seg in range(8):
                gather(lg[:, :], GLv, IL + seg, ['GL'], ['lg'], 'gl')
                if seg == 0:
                    dve(lambda e: e.tensor_tensor_scan(out=cq[0:4, :], data0=onesq[0:4, :], data1=lg[0:4, :], initial=0.0, op0=ALU.mult, op1=ALU.add), ['lg', 'onesq'], ['cq'])
                else:
                    dve(lambda e: e.tensor_tensor_scan(out=cq[0:4, :], data0=onesq[0:4, :], data1=lg[0:4, :], initial=cprev[0:4, 0:1], op0=ALU.mult, op1=ALU.add), ['lg', 'onesq', 'cprev'], ['cq'])
                dve(lambda e: e.tensor_copy(out=cprev[0:4, 0:1], in_=cq[0:4, 1023:1024]), ['cq'], ['cprev'])
                act(lambda e, seg=seg: e.activation(out=cr16[0:4, seg * 1024:(seg + 1) * 1024], in_=cq[0:4, :], func=AF.Copy), ['cq'], ['A1'])
                pp, pk = nextps()
                for b in range(8):
                    pe(lambda e, b=b, pp=pp: e.transpose(out=pp[:, b * 4:(b + 1) * 4], in_=cq[0:4, b * 128:(b + 1) * 128], identity=I32t[0:4, 0:4]), ['cq', 'I32t'], [pk])
                dve(lambda e, seg=seg, pp=pp: e.tensor_scalar(out=negct[:, seg * 8:(seg + 1) * 8, :], in0=pp[:, 0:32].rearrange("p (a b) -> p a b", a=8), scalar1=-1.0, scalar2=None, op0=ALU.mult), [pk], ['negct'])
            if FX < 4:
                return
            n = 0
            s2keys = []
            sgen = None
            if SAMP:
                wfence()
                sgen = s_fox_attention()
            for hh in range(4):
                for t16 in range(16):
                    Q, qk = Qp[n % 2], 'Qp%d' % (n % 2)
                    gather(Q[:, :], G1.ap(), IQ + hh * 16 + t16, ['G1'], [qk], qk)
                    spd(lambda e, Q=Q, hh=hh, t16=t16: e.dma_start(out=Q[64:65, :], in_=cr16[hh:hh + 1, t16 * TT:(t16 + 1) * TT]), ['A1'], [qk], qk + 'r')
                    acc, ak = psb[6 + n % 2], 'ps%d' % (6 + n % 2)
                    nblk = 4 * t16 + 4
                    for j in range(nblk):
                        r = j - 4 * t16
                        diag = r >= 0
                        c0 = 128 * r if diag else 0
                        pp, pk = nextps()
                        pe(lambda e, pp=pp, j=j, Q=Q, c0=c0, diag=diag: e.matmul(pp[:, c0:TT], lhsT=Kp[0:65, j * 128:(j + 1) * 128], rhs=Q[0:65, c0:TT], start=True, stop=not diag), ['A2', qk], [pk])
                        if diag:
                            pe(lambda e, pp=pp, c0=c0: e.matmul(pp[:, c0:c0 + 128], lhsT=Ib[:, :], rhs=trib[:, :], start=False, stop=True), ['Ib', 'trib'], [pk])
                        pb, pbk = ptb[ptc[0] % 3], 'ptb%d' % (ptc[0] % 3)
                        ptc[0] += 1
                        act(lambda e, pb=pb, pp=pp, c0=c0, j=j, hh=hh: e.activation(out=pb[:, c0:TT], in_=pp[:, c0:TT], func=AF.Exp, bias=negct[:, j, hh:hh + 1]), [pk, 'negct'], [pbk])
                        pe(lambda e, acc=acc, pb=pb, c0=c0, j=j, nblk=nblk: e.matmul(acc[0:65, c0:TT], lhsT=Vp[:, j, 0:65], rhs=pb[:, c0:TT], start=(j == 0), stop=(j == nblk - 1)), ['A3', pbk], [ak])
                    dve(lambda e, acc=acc: e.reciprocal(out=rr[64:65, :], in_=acc[64:65, :]), [ak], ['rr'])
                    pb2, pb2k = nextps()
                    pe(lambda e, pb2=pb2: e.matmul(pb2[0:64, :], lhsT=ones32[64:65, 0:64], rhs=rr[64:65, :], start=True, stop=True), ['ones32', 'rr'], [pb2k])
                    act(lambda e, pb2=pb2: e.activation(out=bcs[0:64, :], in_=pb2[0:64, :], func=AF.Copy), [pb2k], ['bcs'])
                    os_, osk = ostage[n % 2], 'ostage%d' % (n % 2)
                    dve(lambda e, acc=acc, os_=os_: e.tensor_tensor(out=os_[:, :], in0=acc[0:64, :], in1=bcs[0:64, :], op=ALU.mult), [ak, 'bcs'], [osk])
                    spd(lambda e, os_=os_, hh=hh, t16=t16: e.dma_start(out=send2v[hh * 64:(hh + 1) * 64, t16, :], in_=os_[:, :]), [osk], ['s2_%d' % n], osk)
                    s2keys.append('s2_%d' % n)
                    n += 1
                    if sgen is not None:
                        for _ in range(5):
                            next(sgen, None)
            if sgen is not None:
                for _ in sgen:
                    pass
            if FX < 5:
                return
            allgather(send2, G2, s2keys, ['G2'], 'cc2', 4)
            for t in range(NT):
                load_x(xres_v, t, False)
                for c in range(8):
                    gather(ot[:, c, :], G2.ap(), IO + t * 8 + c, ['G2'], ['A2', 'A2o'], 'go%d' % (c % 2))
                spd(lambda e, t=t: e.dma_start(out=gate[:, :, :], in_=Gs_v[:, :, t * TT:(t + 1) * TT]), ['Gs%d' % t], ['A1'], 'A1')
                dve(lambda e: e.tensor_tensor(out=ot[:, :, :], in0=ot[:, :, :], in1=gate[:, :, :], op=ALU.mult), ['A1', 'A2o'], ['A2o'])
                for c in range(8):
                    pp, pk = nextps()
                    for k in range(8):
                        pe(lambda e, c=c, k=k, pp=pp: e.matmul(pp[:, :], lhsT=WO[:, k, c * 128:(c + 1) * 128], rhs=ot[:, k, :], start=(k == 0), stop=(k == 7)), ['WO', 'A2o'], [pk])
                    dve(lambda e, c=c, pp=pp: e.tensor_tensor(out=xt[:, c, :], in0=pp[:, :], in1=xt[:, c, :], op=ALU.add), [pk, 'xt'], ['xt'])
                store_x(t)
            if SAMP:
                s_after_attention()


        xs = sb("xs", [NS, D], F32); stok = sb("stok", [NS, D], F32); ktok = sb("ktok", [NS, 512], F32)
        gtok = sb("gtok", [NS, D], F32)
        xnTs = sb("xnTs", [128, 8, NS], BF16); gTs = sb("gTs", [128, 8, NS], BF16); oTs = sb("oTs", [128, 8, NS], BF16)
        rs_ = sb("rs_", [NS, 4], F32)
        Qblk = sb("Qblk", [128, 16, NS], BF16)
        bdm = sb("bdm", [16, 256], F32)
        esc = sb("esc", [16, 1], F32); fbb = sb("fbb", [NS, 16], F32); lfn = sb("lfn", [NS, 16], F32)
        ohs = sb("ohs", [33, 128], F32); Bs = sb("Bs", [128, 16], F32); newb = sb("newb", [128, 16], F32)
        triu = sb("triu", [128, 128], F32); iotaf = sb("iotaf", [128, 1], F32)
        pidx = sb("pidx", [128, NS * 64], I32); pidf = A2[:, 0:2048].bitcast(F32)
        lg64 = sb("lg64", [128, 64], F32); Pb = sb("Pb", [128, 64], BF16); osel = sb("osel", [16, 256], F32); o64 = sb("o64", [16, 64], F32)
        k16 = sb("k16", [NS, 512], BF16)
        Ebc = sb("Ebc", [128, 16], F32)
        Wf = Wa[:, 0:16384].bitcast(F32)
        KPG = [Wa[:, 1024 * i:1024 * (i + 1)].rearrange("p (a b) -> p a b", a=4) for i in range(2)]
        VPG = [Wa[:, 2048 + 1024 * i:2048 + 1024 * (i + 1)].rearrange("p (a b) -> p a b", a=4) for i in range(2)]
        KTt = Wa[:, 4096:5120].rearrange("p (a b) -> p a b", a=8)
        Lg = Wf[:, 2560:3584]; Lhp = Wf[:, 3584:4608]; Sx = Wf[:, 4608:5632]
        spd(lambda e: e.dma_start(out=xs[:, :], in_=xs_in.ap()), [], ['xs'], 'setup')
        spd(lambda e: e.dma_start(out=bdm[:, :], in_=cBD.ap()), [], ['bdm'], 'setup')
        spd(lambda e: e.dma_start(out=ohs[:, :], in_=cOHs.ap()), [], ['ohs'], 'setup')
        spd(lambda e: e.dma_start(out=triu[:, :], in_=cTriU.ap()), [], ['triu'], 'setup')
        spd(lambda e: e.dma_start(out=iotaf[:, :], in_=cIota.ap()), [], ['iotaf'], 'setup')
        spd(lambda e: e.dma_start(out=newb[:, :], in_=cNewB.ap()), [], ['newb'], 'newb')
        spd(lambda e: e.dma_start(out=esc[:, :], in_=sinks.ap().rearrange("o h -> h o"), allow_slow_non_contiguous=True), [], ['esc'], 'setup')
        act(lambda e: e.activation(out=esc[:, :], in_=esc[:, :], func=AF.Exp), ['esc'], ['esc'])
        spd(lambda e: e.dma_start(out=fbb[:, :], in_=fbias.ap().rearrange("h o -> o h").partition_broadcast(NS), allow_slow_non_contiguous=True), [], ['fbb'], 'setup')
        spd(lambda e: e.dma_start(out=pidx[:, :], in_=ptab.ap().partition_broadcast(128)), [], ['pidx'], 'setup')
        dve(lambda e: e.tensor_copy(out=pidf[:, :], in_=pidx[:, :]), ['pidx'], ['A2'])
        dve(lambda e: e.tensor_scalar(out=pidf[:, :], in0=pidf[:, :], scalar1=128.0, scalar2=iotaf[:, 0:1], op0=ALU.mult, op1=ALU.add), ['A2', 'iotaf'], ['A2'])
        dve(lambda e: e.tensor_copy(out=pidx[:, :], in_=pidf[:, :]), ['A2'], ['pidx'])

        def s_norm(layer):
            dve(lambda e: e.tensor_tensor(out=stok[:, :], in0=xs[:, :], in1=xs[:, :], op=ALU.mult), ['xs'], ['stok'])
            dve(lambda e: e.reduce_sum(out=rs_[:, 0:1], in_=stok[:, :], axis=AX.X), ['stok'], ['rs_'])
            dve(lambda e: e.tensor_scalar(out=rs_[:, 0:1], in0=rs_[:, 0:1], scalar1=1.0 / D, scalar2=EPS, op0=ALU.mult, op1=ALU.add), ['rs_'], ['rs_'])
            act(lambda e: e.activation(out=rs_[:, 0:1], in_=rs_[:, 0:1], func=AF.Ln), ['rs_'], ['rs_'])
            act(lambda e: e.activation(out=rs_[:, 0:1], in_=rs_[:, 0:1], func=AF.Exp, scale=-0.5), ['rs_'], ['rs_'])
            dve(lambda e: e.tensor_scalar(out=stok[:, :], in0=xs[:, :], scalar1=rs_[:, 0:1], scalar2=None, op0=ALU.mult), ['xs', 'rs_'], ['stok'])
            s_transpose(stok, 'stok', lambda c, pp, pk: dve(lambda e: e.tensor_scalar(out=xnTs[:, c, :], in0=pp[:, c * NS:(c + 1) * NS], scalar1=gt[:, layer, c:c + 1], scalar2=None, op0=ALU.mult), [pk, 'gt'], ['xnTs']))

        def s_transpose(src, skey, evac):
            pp, pk = nextps()
            for c in range(8):
                pe(lambda e, c=c, pp=pp: e.transpose(out=pp[:, c * NS:(c + 1) * NS], in_=src[:, c * 128:(c + 1) * 128], identity=I32t[0:NS, 0:NS]), [skey, 'I32t'], [pk])
            for c in range(8):
                evac(c, pp, pk)

        def s_proj_tok(col0, ncol, evac, wt=None, wkey='W'):
            pp, pk = nextps()
            for k in range(8):
                r = W[:, k, col0:col0 + ncol] if wt is None else wt(k)
                pe(lambda e, k=k, pp=pp, r=r: e.matmul(pp[0:NS, 0:ncol], lhsT=xnTs[:, k, :], rhs=r, start=(k == 0), stop=(k == 7)), [wkey, 'xnTs'], [pk])
            evac(pp, pk)

        def s_gate_feat(col0):
            for c in range(8):
                pp, pk = nextps()
                for k in range(8):
                    pe(lambda e, c=c, k=k, pp=pp: e.matmul(pp[:, 0:NS], lhsT=W[:, k, col0 + c * 128:col0 + (c + 1) * 128], rhs=xnTs[:, k, :], start=(k == 0), stop=(k == 7)), ['W', 'xnTs'], [pk])
                act(lambda e, c=c, pp=pp: e.activation(out=gTs[:, c, :], in_=pp[:, 0:NS], func=AF.Silu), [pk], ['gTs'])

        def s_outproj():
            for half in range(2):
                pp, pk = nextps()
                for k in range(8):
                    pe(lambda e, k=k, pp=pp, half=half: e.matmul(pp[0:NS, :], lhsT=oTs[:, k, :], rhs=WO[:, k, half * 512:(half + 1) * 512], start=(k == 0), stop=(k == 7)), ['WO', 'oTs'], [pk])
                dve(lambda e, pp=pp, half=half: e.tensor_tensor(out=xs[:, half * 512:(half + 1) * 512], in0=pp[0:NS, :], in1=xs[:, half * 512:(half + 1) * 512], op=ALU.add), [pk, 'xs'], ['xs'])

        def s_pool(layer, j):
            s_norm(layer)
            for half in range(2):
                s_proj_tok(half * 512, 512, lambda pp, pk, half=half: act(lambda e: e.activation(out=stok[:, half * 512:(half + 1) * 512], in_=pp[0:NS, :], func=AF.Copy), [pk], ['stok']))
            s_gate_feat(D)
            spd(lambda e: e.dma_start(out=psS[j, :, 0:14, :], in_=stp[j, :, 1:15, :]), [], [], 'psS%d' % j)
            spd(lambda e: e.dma_start(out=psS[j, :, 14, :], in_=stok[:, :]), ['stok'], [], 'stok')
            stg = Wf[0:NS, 0:3840].rearrange("p (r c) -> p r c", r=15)
            for g in range(4):
                w = 2 << g
                nr = w - 1
                spd(lambda e, g=g, nr=nr: e.dma_start(out=stg[:, 0:nr, :], in_=stp[j, :, 15 - nr:15, g * 256:(g + 1) * 256]), [], ['W'], 'stg')
                if nr == 1:
                    dve(lambda e, g=g: e.tensor_tensor(out=gtok[:, g * 256:(g + 1) * 256], in0=stg[:, 0, :], in1=stok[:, g * 256:(g + 1) * 256], op=ALU.add), ['W', 'stok'], ['gtok'])
                else:
                    dve(lambda e, g=g, nr=nr: e.reduce_sum(out=gtok[:, g * 256:(g + 1) * 256], in_=stg[:, 0:nr, :].rearrange("p r c -> p c r"), axis=AX.X), ['W'], ['gtok'])
                    dve(lambda e, g=g: e.tensor_tensor(out=gtok[:, g * 256:(g + 1) * 256], in0=gtok[:, g * 256:(g + 1) * 256], in1=stok[:, g * 256:(g + 1) * 256], op=ALU.add), ['gtok', 'stok'], ['gtok'])
                dve(lambda e, g=g, w=w: e.scalar_tensor_tensor(out=gtok[:, g * 256:(g + 1) * 256], in0=gtok[:, g * 256:(g + 1) * 256], scalar=1.0 / w, in1=stok[:, g * 256:(g + 1) * 256], op0=ALU.mult, op1=ALU.subtract), ['gtok', 'stok'], ['gtok'])
            pTs = oTs
            s_transpose(gtok, 'gtok', lambda c, pp, pk: act(lambda e: e.activation(out=pTs[:, c, :], in_=pp[:, c * NS:(c + 1) * NS], func=AF.Copy), [pk], ['oTs']))
            pm, pmk = nextps()
            for c in range(8):
                g, half = c // 2, c % 2
                for kc in range(2):
                    pe(lambda e, c=c, g=g, half=half, kc=kc: e.matmul(pm[:, c * NS:(c + 1) * NS], lhsT=w_mix[:, g, kc, half * 128:(half + 1) * 128], rhs=pTs[:, 2 * g + kc, :], start=(kc == 0), stop=(kc == 1)), ['w_mix', 'oTs'], [pmk])
            for c in range(8):
                dve(lambda e, c=c: e.scalar_tensor_tensor(out=xnTs[:, c, :], in0=pm[:, c * NS:(c + 1) * NS], scalar=psc[:, j, c:c + 1], in1=gTs[:, c, :], op0=ALU.mult, op1=ALU.mult), [pmk, 'psc', 'gTs'], ['xnTs'])
            dve(lambda e: e.tensor_copy(out=oTs[:, :, :], in_=xnTs[:, :, :]), ['xnTs'], ['oTs'])
            s_outproj()

        def s_qblk():
            for hp in range(2):
                qpad = Wf[0:NS, 0:1024].rearrange("p (h c) -> p h c", h=8)
                dve(lambda e: e.memset(Wf[0:NS, 0:1024], 0.0), [], ['W'])
                for hl in range(8):
                    h = hp * 8 + hl
                    kvl = (h // 4) % 2
                    dve(lambda e, hl=hl, h=h, kvl=kvl: e.tensor_copy(out=qpad[:, hl, kvl * 64:(kvl + 1) * 64], in_=stok[:, h * 64:(h + 1) * 64]), ['stok'], ['W'])
                pp, pk = nextps()
                for hl in range(8):
                    pe(lambda e, hl=hl, pp=pp: e.transpose(out=pp[:, hl * NS:(hl + 1) * NS], in_=qpad[:, hl, :], identity=I32t[0:NS, 0:NS]), ['W', 'I32t'], [pk])
                act(lambda e, hp=hp, pp=pp: e.activation(out=Qblk[:, hp * 8:(hp + 1) * 8, :], in_=pp[:, 0:8 * NS].rearrange("p (a b) -> p a b", a=8), func=AF.Copy), [pk], ['Qblk'])

        def s_attend_chunk(s_, kpg, vpg, kkey, vkey, npg, bias_ap, bias_keys, accO, accD, first, last):
            pT, pTk = nextps()
            pTb = pT[:, :].bitcast(BF16)
            for pg in range(npg):
                for pair in range(2):
                    pe(lambda e, pg=pg, pair=pair: e.transpose(out=pTb[:, (pg * 2 + pair) * 128:(pg * 2 + pair + 1) * 128], in_=kpg[:, pg, pair * 128:(pair + 1) * 128], identity=Ib[:, :]), [kkey, 'Ib'], [pTk])
            dve(lambda e: e.tensor_copy(out=KTt[:, 0:2 * npg, :], in_=pTb[:, 0:256 * npg].rearrange("p (a b) -> p a b", a=2 * npg)), [pTk], ['KTt'])
            pS, pSk = nextps()
            for pg in range(npg):
                for pair in range(2):
                    pe(lambda e, pg=pg, pair=pair: e.matmul(pS[:, pg * 16 + pair * 8:pg * 16 + pair * 8 + 8], lhsT=KTt[:, pg * 2 + pair, :], rhs=Qblk[:, pair * 8:(pair + 1) * 8, s_], start=True, stop=True), ['KTt', 'Qblk'], [pSk])
            dve(lambda e: e.tensor_tensor(out=lg64[:, 0:16 * npg].rearrange("p (a b) -> p a b", a=npg), in0=pS[:, 0:16 * npg].rearrange("p (a b) -> p a b", a=npg), in1=bias_ap, op=ALU.add), [pSk] + bias_keys, ['lg64'])
            act(lambda e: e.activation(out=Pb[:, 0:16 * npg], in_=lg64[:, 0:16 * npg], func=AF.Exp), ['lg64'], ['Pb'])
            for pg in range(npg):
                st_, sp_ = first and pg == 0, last and pg == npg - 1
                pe(lambda e, pg=pg, st_=st_, sp_=sp_: e.matmul(accO[0:16, 0:256], lhsT=Pb[:, pg * 16:(pg + 1) * 16], rhs=vpg[:, pg, :], start=st_, stop=sp_), ['Pb', vkey], ['ps4'])
                pe(lambda e, pg=pg, st_=st_, sp_=sp_: e.matmul(accD[0:16, 0:1], lhsT=Pb[:, pg * 16:(pg + 1) * 16], rhs=ones[:, 0:1], start=st_, stop=sp_), ['Pb', 'ones'], ['ps5'])

        def s_attend_finish(s_, accO, accD, use_sink):
            if use_sink:
                dve(lambda e: e.tensor_tensor(out=rs_[:, 1:2], in0=accD[0:16, 0:1], in1=esc[:, 0:1], op=ALU.add), ['ps5', 'esc'], ['rs_'])
            else:
                dve(lambda e: e.tensor_copy(out=rs_[:, 1:2], in_=accD[0:16, 0:1]), ['ps5'], ['rs_'])
            dve(lambda e: e.reciprocal(out=rs_[:, 1:2], in_=rs_[:, 1:2]), ['rs_'], ['rs_'])
            dve(lambda e: e.tensor_tensor(out=osel[:, :], in0=accO[0:16, 0:256], in1=bdm[:, :], op=ALU.mult), ['ps4', 'bdm'], ['osel'])
            dve(lambda e: e.reduce_sum(out=o64[:, :], in_=osel[:, :].rearrange("p (k d) -> p d k", k=4), axis=AX.X), ['osel'], ['o64'])
            dve(lambda e: e.tensor_scalar(out=o64[:, :], in0=o64[:, :], scalar1=rs_[:, 1:2], scalar2=None, op0=ALU.mult), ['o64', 'rs_'], ['o64'])
            spd(lambda e: e.dma_start(out=osamp[s_, :].rearrange("(h d) -> h d", h=16), in_=o64[:, :]), ['o64'], ['osamp%d' % s_], 'o64')

        def s_after_attention():
            spd(lambda e: e.dma_start(out=gtok[:, :], in_=osamp.ap()), ['osamp%d' % i for i in range(NS)], ['gtok'], 'gtok')
            s_transpose(gtok, 'gtok', lambda c, pp, pk: dve(lambda e: e.tensor_tensor(out=oTs[:, c, :], in0=pp[:, c * NS:(c + 1) * NS], in1=gTs[:, c, :], op=ALU.mult), [pk, 'gTs'], ['oTs']))
            s_outproj()

        def s_qkv(qc, kfn, vc, wkey_k):
            for half in range(2):
                s_proj_tok(qc + half * 512, 512, lambda pp, pk, half=half: act(lambda e: e.activation(out=stok[:, half * 512:(half + 1) * 512], in_=pp[0:NS, :], func=AF.Copy, scale=0.125), [pk], ['stok']))
            for pair in range(2):
                s_proj_tok(0, 128, lambda pp, pk, pair=pair: act(lambda e: e.activation(out=ktok[:, pair * 128:(pair + 1) * 128], in_=pp[0:NS, 0:128], func=AF.Copy), [pk], ['ktok']), wt=lambda k, pair=pair: kfn(k, pair), wkey=wkey_k)
            s_proj_tok(vc, 256, lambda pp, pk: act(lambda e: e.activation(out=ktok[:, 256:512], in_=pp[0:NS, 0:256], func=AF.Copy), [pk], ['ktok']))
            dve(lambda e: e.tensor_copy(out=k16[:, :], in_=ktok[:, :]), ['ktok'], ['k16'])

        def s_swa(layer):
            s_norm(layer)
            for half in range(2):
                s_proj_tok(half * 512, 512, lambda pp, pk, half=half: act(lambda e: e.activation(out=stok[:, half * 512:(half + 1) * 512], in_=pp[0:NS, :], func=AF.Copy, scale=0.125), [pk], ['stok']))
            for kv in range(4):
                s_proj_tok(0, 64, lambda pp, pk, kv=kv: act(lambda e: e.activation(out=ktok[:, kv * 64:(kv + 1) * 64], in_=pp[0:NS, 0:64], func=AF.Copy), [pk], ['ktok']), wt=lambda k, kv=kv: wkd[:, k, kv, 0:64], wkey='wkd')
            s_proj_tok(1024, 256, lambda pp, pk: act(lambda e: e.activation(out=ktok[:, 256:512], in_=pp[0:NS, 0:256], func=AF.Copy), [pk], ['ktok']))
            dve(lambda e: e.tensor_copy(out=k16[:, :], in_=ktok[:, :]), ['ktok'], ['k16'])
            s_gate_feat(1280)
            s_qblk()
            spd(lambda e: e.dma_start(out=wkS[:, 0:127, :], in_=cwk[:, 1:128, :]), [], [], 'wkS')
            spd(lambda e: e.dma_start(out=wvS[:, 0:127, :], in_=cwv[:, 1:128, :]), [], [], 'wvS')
            spd(lambda e: e.dma_start(out=wkS[:, 127, :], in_=ktok[:, 0:256]), ['ktok'], [], 'ktokA')
            spd(lambda e: e.dma_start(out=wvS[:, 127, :], in_=ktok[:, 256:512]), ['ktok'], [], 'ktokB')
            pp, pk = nextps()
            pe(lambda e: e.matmul(pp[:, 0:16], lhsT=ohs[:, :], rhs=rbx[:, :], start=True, stop=True), ['ohs', 'rbx'], [pk])
            act(lambda e: e.activation(out=Bs[:, :], in_=pp[:, 0:16], func=AF.Copy), [pk], ['Bs'])
            wfence()
            for s_ in range(NS):
                kp, vp = KPG[s_ % 2], VPG[s_ % 2]
                kk_, vk_ = 'KPG%d' % (s_ % 2), 'VPG%d' % (s_ % 2)
                pld(lambda e, s_=s_, kp=kp: e.dma_start(out=kp[:, 0, :], in_=cwk[s_, :, :]), [], [kk_], kk_)
                pld(lambda e, s_=s_, vp=vp: e.dma_start(out=vp[:, 0, :], in_=cwv[s_, :, :]), [], [vk_], vk_)
                spd(lambda e, s_=s_, kp=kp: e.dma_start(out=kp[0:1, 0, :], in_=k16[s_:s_ + 1, 0:256]), ['k16'], [kk_], kk_ + 'n')
                spd(lambda e, s_=s_, vp=vp: e.dma_start(out=vp[0:1, 0, :], in_=k16[s_:s_ + 1, 256:512]), ['k16'], [vk_], vk_ + 'n')
                s_attend_chunk(s_, kp, vp, kk_, vk_, 1, Bs[:, :].rearrange("p (a b) -> p a b", a=1), ['Bs'], psb[4], psb[5], True, True)
                s_attend_finish(s_, psb[4], psb[5], True)
            s_after_attention()

        def s_fox_pre(layer):
            s_norm(layer)
            s_qkv(0, lambda k, pair: wkd[:, k, pair, :], 1024, 'wkd')
            s_gate_feat(1296)
            s_proj_tok(1280, 16, lambda pp, pk: act(lambda e: e.activation(out=lfn[:, :], in_=pp[0:NS, 0:16], func=AF.Copy), [pk], ['lfn']))
            dve(lambda e: e.tensor_tensor(out=lfn[:, :], in0=lfn[:, :], in1=fbb[:, :], op=ALU.add), ['lfn', 'fbb'], ['lfn'])
            act(lambda e: e.activation(out=lfn[:, :], in_=lfn[:, :], func=AF.Exp, scale=-1.0), ['lfn'], ['lfn'])
            dve(lambda e: e.tensor_scalar(out=lfn[:, :], in0=lfn[:, :], scalar1=1.0, scalar2=None, op0=ALU.add), ['lfn'], ['lfn'])
            act(lambda e: e.activation(out=lfn[:, :], in_=lfn[:, :], func=AF.Ln), ['lfn'], ['lfn'])
            dve(lambda e: e.tensor_scalar(out=lfn[:, :], in0=lfn[:, :], scalar1=-1.0, scalar2=None, op0=ALU.mult), ['lfn'], ['lfn'])
            spd(lambda e: e.dma_start(out=fkS.ap(), in_=ktok[:, 0:256]), ['ktok'], [], 'ktokA')
            spd(lambda e: e.dma_start(out=fvS.ap(), in_=ktok[:, 256:512]), ['ktok'], [], 'ktokB')
            spd(lambda e: e.dma_start(out=flS.ap(), in_=lfn[:, :]), ['lfn'], [], 'lfn')
            s_qblk()

        WSCR = ['KPG0', 'KPG1', 'VPG0', 'VPG1', 'KTt', 'Lg', 'Lhp', 'Sx']

        def wfence():
            P.op('dve', lambda e: e.memset(fdummy[:, :], 0.0), R=['W'], W=WSCR + ['fdummy'])

        def s_fox_attention():
            Sv = Sx[:, :].rearrange("p (h g) -> p h g", h=16)
            for s_ in range(NS):
                for pg in range(64):
                    P.op('pool', lambda e, pg=pg, s_=s_: e.indirect_dma_start(out=Lg[:, pg * 16:(pg + 1) * 16], out_offset=None, in_=cfl.ap(), in_offset=bass.IndirectOffsetOnAxis(ap=pidx[:, s_ * 64 + pg:s_ * 64 + pg + 1], axis=0)), R=['pidx'], W=['Lg'], dma='glf%d' % (pg % 2))
                dve(lambda e: e.tensor_copy(out=Lhp[:, :].rearrange("p (h g) -> p h g", h=16), in_=Lg[:, :].rearrange("p (g h) -> p h g", h=16)), ['Lg'], ['Lhp'])
                pT_ = [nextps() for _ in range(2)]
                pW_ = [nextps() for _ in range(2)]
                for half in range(2):
                    pe(lambda e, half=half: e.matmul(pT_[half][0][:, :], lhsT=ones32[:, :], rhs=Lhp[:, half * 512:(half + 1) * 512], start=True, stop=True), ['ones32', 'Lhp'], [pT_[half][1]])
                    pe(lambda e, half=half: e.matmul(pW_[half][0][:, :], lhsT=triu[:, :], rhs=Lhp[:, half * 512:(half + 1) * 512], start=True, stop=True), ['triu', 'Lhp'], [pW_[half][1]])
                dve(lambda e: e.tensor_tensor_scan(out=Sx[:, 0:512], data0=onesq[:, 0:512], data1=pT_[0][0][:, :], initial=0.0, op0=ALU.mult, op1=ALU.add), [pT_[0][1], 'onesq'], ['Sx'])
                dve(lambda e: e.tensor_tensor_scan(out=Sx[:, 512:1024], data0=onesq[:, 0:512], data1=pT_[1][0][:, :], initial=Sx[:, 511:512], op0=ALU.mult, op1=ALU.add), [pT_[1][1], 'onesq', 'Sx'], ['Sx'])
                dve(lambda e: e.tensor_copy(out=Ebc[:, :], in_=Sv[:, :, 63]), ['Sx'], ['Ebc'])
                spd(lambda e, s_=s_: e.dma_start(out=newb[0:1, :], in_=lfn[s_:s_ + 1, :]), ['lfn'], ['newb'], 'newb')
                dve(lambda e: e.tensor_scalar(out=newb[0:1, :], in0=newb[0:1, :], scalar1=-1.0, scalar2=None, op0=ALU.mult), ['newb'], ['newb'])
                for half in range(2):
                    dve(lambda e, half=half: e.tensor_tensor(out=Sx[:, half * 512:(half + 1) * 512], in0=pT_[half][0][:, :], in1=Sx[:, half * 512:(half + 1) * 512], op=ALU.subtract), [pT_[half][1], 'Sx'], ['Sx'])
                    dve(lambda e, half=half: e.tensor_tensor(out=Sx[:, half * 512:(half + 1) * 512], in0=Sx[:, half * 512:(half + 1) * 512], in1=pW_[half][0][:, :], op=ALU.subtract), [pW_[half][1], 'Sx'], ['Sx'])
                dve(lambda e: e.tensor_tensor(out=Sv, in0=Sv, in1=Ebc[:, :].unsqueeze(2).to_broadcast([128, 16, 64]), op=ALU.add), ['Sx', 'Ebc'], ['Sx'])
                yield
                for ci in range(17):
                    npg = 4 if ci < 16 else 1
                    kp, vp = KPG[ci % 2], VPG[ci % 2]
                    kk_, vk_ = 'KPG%d' % (ci % 2), 'VPG%d' % (ci % 2)
                    if ci < 16:
                        for pgi in range(4):
                            pg = ci * 4 + pgi
                            P.op('pool', lambda e, pg=pg, pgi=pgi, kp=kp, s_=s_: e.indirect_dma_start(out=kp[:, pgi, :], out_offset=None, in_=cfk.ap(), in_offset=bass.IndirectOffsetOnAxis(ap=pidx[:, s_ * 64 + pg:s_ * 64 + pg + 1], axis=0)), R=['pidx'], W=[kk_], dma=kk_ + 'g')
                            P.op('pool', lambda e, pg=pg, pgi=pgi, vp=vp, s_=s_: e.indirect_dma_start(out=vp[:, pgi, :], out_offset=None, in_=cfv.ap(), in_offset=bass.IndirectOffsetOnAxis(ap=pidx[:, s_ * 64 + pg:s_ * 64 + pg + 1], axis=0)), R=['pidx'], W=[vk_], dma=vk_ + 'g')
                        bias_ap, bkeys = Sv[:, :, ci * 4:(ci + 1) * 4].rearrange("p h g -> p g h"), ['Sx']
                    else:
                        dve(lambda e, kp=kp: e.memset(kp[:, 0, :], 0.0), [], [kk_])
                        dve(lambda e, vp=vp: e.memset(vp[:, 0, :], 0.0), [], [vk_])
                        spd(lambda e, s_=s_, kp=kp: e.dma_start(out=kp[0:1, 0, :], in_=k16[s_:s_ + 1, 0:256]), ['k16'], [kk_], kk_ + 'n')
                        spd(lambda e, s_=s_, vp=vp: e.dma_start(out=vp[0:1, 0, :], in_=k16[s_:s_ + 1, 256:512]), ['k16'], [vk_], vk_ + 'n')
                        bias_ap, bkeys = newb[:, :].rearrange("p (a b) -> p a b", a=1), ['newb']
                    s_attend_chunk(s_, kp, vp, kk_, vk_, npg, bias_ap, bkeys, psb[4], psb[5], ci == 0, ci == 16)
                    yield
                s_attend_finish(s_, psb[4], psb[5], False)

        def s_final():
            dve(lambda e: e.tensor_tensor(out=stok[:, :], in0=xs[:, :], in1=xs[:, :], op=ALU.mult), ['xs'], ['stok'])
            dve(lambda e: e.reduce_sum(out=rs_[:, 0:1], in_=stok[:, :], axis=AX.X), ['stok'], ['rs_'])
            dve(lambda e: e.tensor_scalar(out=rs_[:, 0:1], in0=rs_[:, 0:1], scalar1=1.0 / D, scalar2=EPS, op0=ALU.mult, op1=ALU.add), ['rs_'], ['rs_'])
            act(lambda e: e.activation(out=rs_[:, 0:1], in_=rs_[:, 0:1], func=AF.Ln), ['rs_'], ['rs_'])
            act(lambda e: e.activation(out=rs_[:, 0:1], in_=rs_[:, 0:1], func=AF.Exp, scale=-0.5), ['rs_'], ['rs_'])
            dve(lambda e: e.tensor_scalar(out=stok[:, :], in0=xs[:, :], scalar1=rs_[:, 0:1], scalar2=None, op0=ALU.mult), ['xs', 'rs_'], ['stok'])
            yst = Wf[:, 0:128].rearrange("p (c s) -> p c s", c=8)
            s_transpose(stok, 'stok', lambda c, pp, pk: dve(lambda e: e.tensor_scalar(out=yst[:, c, :], in0=pp[:, c * NS:(c + 1) * NS], scalar1=gt[:, 4, c:c + 1], scalar2=None, op0=ALU.mult), [pk, 'gt'], ['W']))
            spd(lambda e: e.dma_start(out=ysT.ap(), in_=yst), ['W'], [], 'yst')

        def debug_out():
            for t in range(1, NT):
                load_x(xres_v, t, False)
                spd(lambda e, t=t: e.dma_start(out=yT_v[:, :, (t - 1) * TT:t * TT], in_=xt[:, :, :]), ['xt'], [], 'xt')

        nl = int(os.environ.get('KDBG_LAYERS', '9'))
        SAMP = int(os.environ.get('KDBG_SAMP', '1'))
        pool_layer(0, 0, True, False)
        if SAMP:
            s_pool(0, 0)
        if nl >= 2:
            swa_layer(1)
            if SAMP:
                s_swa(1)
        if nl >= 3:
            fox_layer(2)
        if nl >= 4:
            pool_layer(3, 1, False, True)
            if SAMP:
                s_pool(3, 1)
                s_final()
        if nl < 4:
            debug_out()
        P.emit()
    return nc


def _t5_bucket_np(d):
    n = np.maximum(d, 0)
    nf = np.maximum(n, 1).astype(np.float32)
    large = 16 + (np.log(nf / 16) / np.log(128 / 16) * 16).astype(np.int32)
    large = np.minimum(large, 31)
    return np.where(n < 16, n, large)


def kernel(x_prompt, x_sample, state_pool, cache_win_k, cache_win_v, cache_fox_k, cache_fox_v, cache_fox_logf, page_table,
           norm_g, final_norm_g, rel_bias, pool_w_in, pool_mix, pool_scale, pool_w_out,
           swa_w_in, swa_sinks, swa_w_out, fox_w_in, fox_f_bias, fox_w_out):
    f32 = np.float32
    nc = build_program()
    gall = np.concatenate([np.asarray(norm_g, f32), np.asarray(final_norm_g, f32)[None]], 0)
    gT = np.ascontiguousarray(gall.reshape(5, 8, 128).transpose(2, 0, 1))
    pscale = np.ascontiguousarray(np.asarray(pool_scale, f32).reshape(2, 8, 128).transpose(2, 0, 1))
    tt = np.arange(16)
    cJ = np.ascontiguousarray(np.eye(128, dtype=f32)[::-1])
    cI = np.eye(128, dtype=f32)
    jj, ii = np.meshgrid(np.arange(128), np.arange(128), indexing='ij')
    cTri = np.where(jj > ii, NEG, 0.0).astype(f32)
    dist = np.arange(384) - 127
    bucket = np.where((dist >= 0) & (dist < 128), _t5_bucket_np(dist), 32)
    cOH = (np.arange(33)[:, None] == bucket[None, :]).astype(f32)
    cSel = np.zeros((NS, NS * 128), f32)
    for k_ in range(NS):
        cSel[k_, k_ * 128:(k_ + 1) * 128] = 1.0
    cBD = (np.arange(16)[:, None] // 4 == (np.arange(256)[None, :] // 64)).astype(f32)
    dpos = np.where(np.arange(128) == 0, 0, 128 - np.arange(128))
    cOHs = (np.arange(33)[:, None] == _t5_bucket_np(dpos)[None, :]).astype(f32)
    cTriU = (np.arange(128)[:, None] <= np.arange(128)[None, :]).astype(f32)
    cIota = np.arange(128, dtype=f32).reshape(128, 1)
    cNewB = np.full((128, 16), NEG, f32)
    cNewB[0] = 0.0
    cfk_full = np.asarray(cache_fox_k, f32).reshape(-1, 256)
    cfv_full = np.asarray(cache_fox_v, f32).reshape(-1, 256)
    cfl_full = np.asarray(cache_fox_logf, f32).reshape(-1, 16)
    in_maps = []
    for c in range(8):
        b, q = c // 4, c % 4
        sl = slice(NS * c, NS * (c + 1))
        kv = q
        xs = np.asarray(x_prompt[b], f32)
        xcols = np.zeros((NCOL, D), f32)
        lo = q * 2048 - TT
        if lo >= 0:
            xcols[:] = xs[lo:lo + NCOL]
        else:
            xcols[TT:] = xs[0:2048]
        invc = np.stack([1.0 / (np.minimum(2 << g, tt + 1) if q == 0 else np.full(16, 2 << g)) for g in range(4)]).astype(f32)
        invc = np.ascontiguousarray(np.broadcast_to(invc[None], (128, 4, 16)))
        hm = np.full((128, 1), NEG if q == 0 else 0.0, f32)
        idx = np.zeros((128, 228), np.int32)
        pp_ = np.arange(128)
        g1row = lambda rank, row: (row // 1024) * 4096 + rank * 1024 + (row % 1024)
        for hh in range(4):
            for t16 in range(16):
                idx[:, hh * 16 + t16] = g1row(t16 // 4, ((4 * kv + hh) * 64 + (pp_ % 64)) * 4 + (t16 % 4))
        for r_ in range(4):
            for t4 in range(4):
                idx[:, 64 + r_ * 4 + t4] = g1row(r_, (1024 + kv * 64 + (pp_ % 64)) * 4 + t4)
        for blk in range(64):
            idx[:, 80 + blk] = (blk // 16) * 8192 + kv * 2048 + (blk % 16) * 128 + pp_
        for seg in range(8):
            idx[:, 144 + seg] = ((seg // 2) * 16 + 4 * kv + (pp_ % 4)) * 2 + (seg % 2)
        for t in range(NT):
            tg = max(4 * q + t - 1, 0)
            for c8 in range(8):
                f_ = c8 * 128 + pp_
                h_ = f_ // 64
                idx[:, 152 + t * 8 + c8] = g1row(h_ // 4, ((h_ % 4) * 64 + (f_ % 64)) * 16 + tg)
        in_maps.append(dict(
            xT=np.ascontiguousarray(xcols.T), gT=gT,
            pw_in=np.asarray(pool_w_in, f32), pmix=np.asarray(pool_mix, f32), pscale=pscale,
            pw_out=np.asarray(pool_w_out, f32), invc=invc,
            sw_in=np.asarray(swa_w_in[0], f32), sw_out=np.asarray(swa_w_out[0], f32),
            sinks=np.asarray(swa_sinks, f32).reshape(1, 16), relb=np.asarray(rel_bias, f32),
            fw_in=np.asarray(fox_w_in[0], f32), fw_out=np.asarray(fox_w_out[0], f32),
            fbias=np.asarray(fox_f_bias, f32).reshape(16, 1),
            cJ=cJ, cI=cI, cTri=cTri, cOH=cOH, hmask=hm, idx=idx,
            xs_in=np.ascontiguousarray(np.asarray(x_sample, f32)[sl, 0, :]),
            stp=np.ascontiguousarray(np.asarray(state_pool, f32)[:, sl]),
            cwk=np.ascontiguousarray(np.asarray(cache_win_k, f32)[0, sl].reshape(NS, 128, 256)),
            cwv=np.ascontiguousarray(np.asarray(cache_win_v, f32)[0, sl].reshape(NS, 128, 256)),
            cfk=cfk_full, cfv=cfv_full, cfl=cfl_full,
            ptab=np.ascontiguousarray(np.asarray(page_table, np.int32)[sl].reshape(1, NS * 64)),
            cBD=cBD, cOHs=cOHs, cTriU=cTriU, cIota=cIota, cNewB=cNewB))
    res = run_bass_kernel_spmd(nc, in_maps, core_ids=list(range(8)))
    r = res.results
    B, S = 2, SEQ
    xfull = [np.concatenate([r[4 * b + q]["xres"] if False else np.zeros((D, 2048), f32) for q in range(4)], 1) for b in range(B)]
    y_prompt = np.stack([np.concatenate([r[4 * b + q]["yT"] for q in range(4)], 1).T for b in range(B)]).astype(f32)
    pool_p = np.stack([np.stack([r[4 * b + 3]["pspT"][j].T for b in range(B)]) for j in range(2)]).astype(f32)
    wk = np.stack([r[4 * b + 3]["wkT"][0:64].transpose(2, 1, 0) for b in range(B)])[None].astype(f32)
    wvv = np.stack([r[4 * b + 3]["wv"].reshape(128, 4, 64) for b in range(B)])[None].astype(f32)
    z = lambda *s: np.zeros(s, f32)
    fk = np.stack([np.concatenate([r[4 * b + q]["fkT"] for q in range(4)], 1).T.reshape(S, 4, 64) for b in range(B)])[None].astype(f32)
    fvv = np.stack([np.concatenate([r[4 * b + q]["fv"] for q in range(4)], 0).reshape(S, 4, 64) for b in range(B)])[None].astype(f32)
    fl = np.stack([np.concatenate([r[4 * b + q]["flT"] for q in range(4)], 1).T for b in range(B)])[None].astype(f32)
    y_sample = np.concatenate([r[c]["ysT"].transpose(2, 1, 0).reshape(NS, D) for c in range(8)], 0).reshape(128, 1, D).astype(f32)
    pool_s = np.concatenate([r[c]["psS"] for c in range(8)], 1).astype(f32)
    wks = np.concatenate([r[c]["wkS"] for c in range(8)], 0).reshape(1, 128, 128, 4, 64).astype(f32)
    wvs = np.concatenate([r[c]["wvS"] for c in range(8)], 0).reshape(1, 128, 128, 4, 64).astype(f32)
    fks = np.concatenate([r[c]["fkS"] for c in range(8)], 0).reshape(1, 128, 1, 4, 64).astype(f32)
    fvs = np.concatenate([r[c]["fvS"] for c in range(8)], 0).reshape(1, 128, 1, 4, 64).astype(f32)
    fls = np.concatenate([r[c]["flS"] for c in range(8)], 0).reshape(1, 128, 1, 16).astype(f32)
    return (y_prompt, y_sample, pool_p, pool_s, wk, wvv, wks, wvs, fk, fvv, fl, fks, fvs, fls)
```
